# Optimizing a Trainium2 kernel written in Bass

```python
import math
import jax, jax.numpy as jnp
from jax import lax
import numpy as np

D_MODEL = 1024
BATCH = 16
SEQ = 2048
DEPTH = 1

ATT_HEADS = 8
ATT_HEAD_DIM = D_MODEL // ATT_HEADS // 2
ATT_V_DIM = 2 * ATT_HEAD_DIM
ATT_WIDTH = ATT_HEADS * ATT_V_DIM
Q_BLOCK = 128
SSM_WIDTH = D_MODEL // 2
SSM_GROUP = 16
SSM_GROUPS = SSM_WIDTH // SSM_GROUP
SSM_STATE = 64
DT_MIN = 1e-3
DT_MAX = 1e-1
N_BRANCHES = 2
SPLIT_SIZES = [ATT_WIDTH] * 4 + [SSM_WIDTH] * 2 + [D_MODEL] * N_BRANCHES
IN_WIDTH = sum(SPLIT_SIZES)
EPS = 1e-5

kernel_name = 'hybrid_diffattn_s5_gated_block'


def rmsnorm(x, g):
    xf = x.astype(jnp.float32)
    y = xf * lax.rsqrt(jnp.mean(xf * xf, axis=-1, keepdims=True) + EPS)
    return (y * g.astype(jnp.float32)).astype(x.dtype)


def alibi_slopes(n_heads):
    return jnp.asarray([2.0 ** (-8.0 * (h + 1) / n_heads) for h in range(n_heads)], dtype=jnp.float32)


def diff_attention(q, k, v, lam, lambda_init, subln_g):
    b, L = q.shape[0], q.shape[1]
    q = jnp.transpose(q, (0, 2, 3, 1, 4)).astype(jnp.float32)
    k = jnp.transpose(k, (0, 2, 3, 1, 4)).astype(jnp.float32)
    v = jnp.transpose(v, (0, 2, 1, 3)).astype(jnp.float32)
    scale = ATT_HEAD_DIM ** -0.5
    slopes = alibi_slopes(ATT_HEADS)[None, :, None, None, None]
    outs = []
    for i in range(L // Q_BLOCK):
        q0, k_end = i * Q_BLOCK, (i + 1) * Q_BLOCK
        qs = q[:, :, :, q0:k_end]
        ks = k[:, :, :, :k_end]
        vs = v[:, :, :k_end]
        s = jnp.einsum('bhcqd,bhckd->bhcqk', qs, ks) * scale
        dist = (jnp.arange(q0, k_end)[:, None] - jnp.arange(k_end)[None, :]).astype(jnp.float32)
        s = jnp.where(dist >= 0, s - slopes * dist, -jnp.inf)
        p = jax.nn.softmax(s, axis=-1)
        pd = p[:, :, 0] - lam * p[:, :, 1]
        outs.append(jnp.einsum('bhqk,bhkv->bhqv', pd, vs))
    o = jnp.concatenate(outs, axis=2)
    o = rmsnorm(o, subln_g) * (1.0 - lambda_init)
    return jnp.transpose(o, (0, 2, 1, 3)).reshape(b, L, ATT_WIDTH)


def s5_branch(u, lam_re, lam_im, log_dt, b_re, b_im, c_re, c_im, d_skip, w_glu, b_glu):
    bsz, L = u.shape[0], u.shape[1]
    uf = u.astype(jnp.float32).reshape(bsz, L, SSM_GROUPS, SSM_GROUP)
    dt = jnp.exp(log_dt.astype(jnp.float32))[:, None]
    lre = jnp.minimum(lam_re.astype(jnp.float32), -1e-4)
    lim = lam_im.astype(jnp.float32)
    mag = jnp.exp(lre * dt)
    lbar_re = mag * jnp.cos(lim * dt)
    lbar_im = mag * jnp.sin(lim * dt)
    num_re = lbar_re - 1.0
    den = lre * lre + lim * lim
    coef_re = ((num_re * lre + lbar_im * lim) / den)[..., None]
    coef_im = ((lbar_im * lre - num_re * lim) / den)[..., None]
    bre = b_re.astype(jnp.float32)
    bim = b_im.astype(jnp.float32)
    bbar_re = coef_re * bre - coef_im * bim
    bbar_im = coef_re * bim + coef_im * bre
    bu_re = jnp.einsum('blgh,gph->blgp', uf, bbar_re)
    bu_im = jnp.einsum('blgh,gph->blgp', uf, bbar_im)
    a_re = jnp.broadcast_to(lbar_re, bu_re.shape)
    a_im = jnp.broadcast_to(lbar_im, bu_im.shape)

    def combine(e1, e2):
        a1r, a1i, b1r, b1i = e1
        a2r, a2i, b2r, b2i = e2
        return (a2r * a1r - a2i * a1i,
                a2r * a1i + a2i * a1r,
                a2r * b1r - a2i * b1i + b2r,
                a2r * b1i + a2i * b1r + b2i)

    _, _, st_re, st_im = lax.associative_scan(combine, (a_re, a_im, bu_re, bu_im), axis=1)
    y = (jnp.einsum('blgp,ghp->blgh', st_re, c_re.astype(jnp.float32))
         - jnp.einsum('blgp,ghp->blgh', st_im, c_im.astype(jnp.float32))
         + d_skip.astype(jnp.float32) * uf)
    y = jax.nn.gelu(y).reshape(bsz, L, SSM_WIDTH)
    y = y * jax.nn.sigmoid(y @ w_glu.astype(jnp.float32) + b_glu.astype(jnp.float32))
    return y.astype(u.dtype)


def setup_inputs(seed: int = 0) -> dict:
    key = jax.random.key(seed)
    ks = jax.random.split(key, 24)
    f32 = jnp.float32
    nrm = lambda k, shape, s: jax.random.normal(k, shape, f32) * s
    n_idx = jnp.arange(SSM_STATE, dtype=f32)
    return {
        'x': jax.random.normal(ks[0], (BATCH, SEQ, D_MODEL), f32),
        'norm_g': 1.0 + nrm(ks[1], (DEPTH, D_MODEL), 0.02),
        'w_in': nrm(ks[2], (DEPTH, D_MODEL, IN_WIDTH), D_MODEL ** -0.5),
        'lambda_q1': nrm(ks[3], (DEPTH, ATT_HEAD_DIM), 0.1),
        'lambda_k1': nrm(ks[4], (DEPTH, ATT_HEAD_DIM), 0.1),
        'lambda_q2': nrm(ks[5], (DEPTH, ATT_HEAD_DIM), 0.1),
        'lambda_k2': nrm(ks[6], (DEPTH, ATT_HEAD_DIM), 0.1),
        'subln_g': 1.0 + nrm(ks[7], (DEPTH, ATT_V_DIM), 0.02),
        'w_o_att': nrm(ks[8], (DEPTH, ATT_WIDTH, D_MODEL), ATT_WIDTH ** -0.5),
        'ssm_lambda_re': -0.5 + nrm(ks[9], (DEPTH, SSM_GROUPS, SSM_STATE), 0.01),
        'ssm_lambda_im': math.pi * n_idx + nrm(ks[10], (DEPTH, SSM_GROUPS, SSM_STATE), 0.01),
        'ssm_log_dt': jax.random.uniform(ks[11], (DEPTH, SSM_GROUPS), f32, math.log(DT_MIN), math.log(DT_MAX)),
        'ssm_b_re': nrm(ks[12], (DEPTH, SSM_GROUPS, SSM_STATE, SSM_GROUP), (2 * SSM_GROUP) ** -0.5),
        'ssm_b_im': nrm(ks[13], (DEPTH, SSM_GROUPS, SSM_STATE, SSM_GROUP), (2 * SSM_GROUP) ** -0.5),
        'ssm_c_re': nrm(ks[14], (DEPTH, SSM_GROUPS, SSM_GROUP, SSM_STATE), SSM_STATE ** -0.5),
        'ssm_c_im': nrm(ks[15], (DEPTH, SSM_GROUPS, SSM_GROUP, SSM_STATE), SSM_STATE ** -0.5),
        'ssm_d': nrm(ks[16], (DEPTH, SSM_GROUPS, SSM_GROUP), 1.0),
        'w_glu': nrm(ks[17], (DEPTH, SSM_WIDTH, SSM_WIDTH), SSM_WIDTH ** -0.5),
        'b_glu': nrm(ks[18], (DEPTH, SSM_WIDTH), 0.01),
        'w_o_ssm': nrm(ks[19], (DEPTH, SSM_WIDTH, D_MODEL), SSM_WIDTH ** -0.5),
        'w_out': nrm(ks[20], (DEPTH, D_MODEL, D_MODEL), D_MODEL ** -0.5),
        'final_g': 1.0 + nrm(ks[21], (D_MODEL,), 0.02),
    }


def reference(x, norm_g, w_in, lambda_q1, lambda_k1, lambda_q2, lambda_k2, subln_g, w_o_att,
              ssm_lambda_re, ssm_lambda_im, ssm_log_dt, ssm_b_re, ssm_b_im, ssm_c_re, ssm_c_im,
              ssm_d, w_glu, b_glu, w_o_ssm, w_out, final_g):
    bsz, L = x.shape[0], x.shape[1]
    split_points = [int(v) for v in np.cumsum(SPLIT_SIZES)[:-1]]
    for l in range(DEPTH):
        lambda_init = 0.8 - 0.6 * math.exp(-0.3 * l)
        h = rmsnorm(x, norm_g[l])
        proj = h @ w_in[l]
        q, k, v, z_att, u, z_ssm, g_att, g_ssm = jnp.split(proj, split_points, axis=-1)
        q = q.reshape(bsz, L, ATT_HEADS, 2, ATT_HEAD_DIM)
        k = k.reshape(bsz, L, ATT_HEADS, 2, ATT_HEAD_DIM)
        v = v.reshape(bsz, L, ATT_HEADS, ATT_V_DIM)
        lam = (jnp.exp(jnp.sum(lambda_q1[l].astype(jnp.float32) * lambda_k1[l].astype(jnp.float32)))
               - jnp.exp(jnp.sum(lambda_q2[l].astype(jnp.float32) * lambda_k2[l].astype(jnp.float32)))
               + lambda_init)
        att = diff_attention(q, k, v, lam, lambda_init, subln_g[l]).astype(x.dtype)
        y_att = (att * jax.nn.silu(z_att)) @ w_o_att[l]
        ssm = s5_branch(u, ssm_lambda_re[l], ssm_lambda_im[l], ssm_log_dt[l], ssm_b_re[l], ssm_b_im[l],
                        ssm_c_re[l], ssm_c_im[l], ssm_d[l], w_glu[l], b_glu[l])
        y_ssm = (ssm * jax.nn.silu(z_ssm)) @ w_o_ssm[l]
        merged = jax.nn.sigmoid(g_att) * y_att + jax.nn.sigmoid(g_ssm) * y_ssm
        x = x + (merged @ w_out[l]).astype(x.dtype)
    return rmsnorm(x, final_g)
```

```python
import contextlib
import math
import numpy as np
import concourse.bass as bass
import concourse.mybir as mybir
from concourse.bass_utils import run_bass_kernel_spmd

F32 = mybir.dt.float32
BF16 = mybir.dt.bfloat16
ALU = mybir.AluOpType
AF = mybir.ActivationFunctionType
AX = mybir.AxisListType

ENGS = ("pe", "act", "dve", "pool", "sp")
SEM_CAP = 1800
L = 2048
D = 1024
NSEQ = 2
EPS = 1e-5


class Op:
    __slots__ = ("eng", "fn", "deps", "signal", "sem", "val", "is_dma", "dkey", "idx")

    def __init__(self, eng, fn, is_dma, dkey):
        self.eng = eng
        self.fn = fn
        self.deps = []
        self.signal = is_dma
        self.sem = None
        self.val = 0
        self.is_dma = is_dma
        self.dkey = dkey


class Sched:
    def __init__(self, nc):
        self.nc = nc
        self.ops = {e: [] for e in ENGS}
        self.last_w = {}
        self.readers = {}
        self.all_ops = []
        self.bar_deps = {e: [] for e in ENGS}
        self.pending_dma = []
        self.finals = []

    def barrier(self):
        deps = []
        for e in ENGS:
            for op in reversed(self.ops[e]):
                if not op.is_dma:
                    deps.append(op)
                    break
        last_per_key = {}
        for op in self.pending_dma:
            last_per_key[op.dkey] = op
        deps.extend(last_per_key.values())
        self.pending_dma = []
        for e in ENGS:
            self.bar_deps[e] = list(deps)
        self.last_w = {}
        self.readers = {}

    def add(self, eng, fn, reads=(), writes=(), dma=None):
        op = Op(eng, fn, dma is not None, dma)
        deps = list(self.bar_deps[eng])
        self.bar_deps[eng] = []
        for r in reads:
            w = self.last_w.get(r)
            if w is not None:
                deps.append(w)
        for r in writes:
            w = self.last_w.get(r)
            if w is not None:
                deps.append(w)
            deps.extend(self.readers.get(r, ()))
        op.idx = len(self.all_ops)
        seen = set()
        latest = {}
        keep = []
        for d in deps:
            if id(d) in seen or d is op:
                continue
            seen.add(id(d))
            if d.eng == "pe" and eng == "pe" and not d.is_dma and dma is None:
                continue
            if not d.is_dma and d.eng in ("pe", "act", "dve"):
                if d.eng not in latest or d.idx > latest[d.eng].idx:
                    latest[d.eng] = d
                continue
            keep.append(d)
        keep.extend(latest.values())
        for d in keep:
            op.deps.append(d)
            d.signal = True
        for r in reads:
            self.readers.setdefault(r, []).append(op)
        for r in writes:
            self.last_w[r] = op
            self.readers[r] = []
        self.ops[eng].append(op)
        self.all_ops.append(op)
        if op.is_dma:
            self.pending_dma.append(op)
        return op

    def emit(self):
        nc = self.nc
        sem_specs = []
        cur = {}
        for op in self.all_ops:
            if not op.signal:
                continue
            key = ("dma", op.dkey) if op.is_dma else ("eng", op.eng)
            if key not in cur or cur[key][1] >= SEM_CAP:
                sem_specs.append("s%d_%s" % (len(sem_specs), str(key[1])[:12]))
                cur[key] = [len(sem_specs) - 1, 0]
            cur[key][1] += 1
            op.sem = cur[key][0]
            op.val = cur[key][1] * (16 if op.is_dma else 1)
        with contextlib.ExitStack() as es:
            sems = [es.enter_context(nc.semaphore(n)) for n in sem_specs]
            block = es.enter_context(nc.Block())
            finals = self.finals

            def run(eng_name):
                def body(eng):
                    waited = {}

                    def w(d):
                        if waited.get(d.sem, 0) >= d.val:
                            return
                        eng.wait_ge(sems[d.sem], d.val)
                        waited[d.sem] = d.val

                    for op in self.ops[eng_name]:
                        for d in op.deps:
                            w(d)
                        ins = op.fn(eng)
                        if op.signal:
                            ins.then_inc(sems[op.sem], 16 if op.is_dma else 1)
                    if eng_name == "sp":
                        fmax = {}
                        for d in finals:
                            if d.sem not in fmax or d.val > fmax[d.sem].val:
                                fmax[d.sem] = d
                        for d in fmax.values():
                            w(d)

                return body

            block.tensor(run("pe"))
            block.scalar(run("act"))
            block.vector(run("dve"))
            block.gpsimd(run("pool"))
            block.sync(run("sp"))
        return len(sem_specs)


def build_program(dbg=False):
    nc = bass.Bass("TRN2", target_bir_lowering=False)
    DBG = {}

    def din(name, shape):
        return nc.dram_tensor(name, shape, F32, kind="ExternalInput").ap()

    x = din("x", [NSEQ, L, D])
    norm_g = din("norm_g", [D])
    w_in = din("w_in", [D, 7168])
    lq1 = din("lambda_q1", [64]); lk1 = din("lambda_k1", [64])
    lq2 = din("lambda_q2", [64]); lk2 = din("lambda_k2", [64])
    subln_g = din("subln_g", [128])
    w_o_att = din("w_o_att", [D, D])
    s_lre = din("ssm_lambda_re", [32, 64]); s_lim = din("ssm_lambda_im", [32, 64])
    s_ldt = din("ssm_log_dt", [32])
    s_bre = din("ssm_b_re", [32, 64, 16]); s_bim = din("ssm_b_im", [32, 64, 16])
    s_cre = din("ssm_c_re", [32, 16, 64]); s_cim = din("ssm_c_im", [32, 16, 64])
    s_d = din("ssm_d", [512])
    w_glu = din("w_glu", [512, 512]); b_glu = din("b_glu", [512])
    w_o_ssm = din("w_o_ssm", [512, D]); w_out = din("w_out", [D, D])
    final_g = din("final_g", [D])
    out = nc.dram_tensor("out", [NSEQ, L, D], F32, kind="ExternalOutput").ap()
    CW = 18432
    cscr = nc.dram_tensor("cscr", [128, CW], F32, kind="Internal").ap()
    if dbg:
        DBG["consts"] = nc.dram_tensor("d_consts", [128, CW], F32, kind="ExternalOutput").ap()
        DBG["hT"] = nc.dram_tensor("d_hT", [128, 8 * L], BF16, kind="ExternalOutput").ap()
        DBG["ssmzT"] = nc.dram_tensor("d_ssmzT", [128, 4 * L], BF16, kind="ExternalOutput").ap()
        DBG["gT"] = nc.dram_tensor("d_gT", [128, 4 * L], BF16, kind="ExternalOutput").ap()
        DBG["attzT"] = nc.dram_tensor("d_attzT", [128, 8 * L], BF16, kind="ExternalOutput").ap()
        DBG["mergedT"] = nc.dram_tensor("d_mergedT", [128, 8 * L], BF16, kind="ExternalOutput").ap()
        DBG["small"] = nc.dram_tensor("d_small", [128, 64], F32, kind="ExternalOutput").ap()
        DBG["bias_tab"] = nc.dram_tensor("d_bias_tab", [128, 128], F32, kind="ExternalOutput").ap()
        DBG["misc"] = nc.dram_tensor("d_misc", [128, 16], F32, kind="ExternalOutput").ap()

    es = contextlib.ExitStack()
    with es:
        def T(name, shape, dt):
            return es.enter_context(nc.sbuf_tensor(name, shape, dt))

        def PS(name, shape, dt):
            return es.enter_context(nc.psum_tensor(name, shape, dt))

        hT = T("hT", [128, 8, L], BF16)
        attzT = T("attzT", [128, 8, L], BF16)
        ssmzT = T("ssmzT", [128, 4, L], BF16)
        wblk = [T("wblk%d" % i, [128, 8, 256], BF16) for i in range(2)]
        ident_f = T("ident_f", [128, 128], F32)
        ident_b = T("ident_b", [128, 128], BF16)
        mask_b = T("mask_b", [128, 128], BF16)
        bias_tab = T("bias_tab", [128, 8, 16], F32)
        g_col = T("g_col", [128, 8], F32)
        sgc = T("sgc", [128, 1], F32)
        nlam = T("nlam", [128, 1], F32)
        bglu_h = T("bglu_h", [128, 4], F32)
        smallf = T("smallf", [128, 64], F32)
        UW = 29440
        U = T("U", [128, UW], F32)

        def carve(off_words, shape, dt):
            n = 1
            for s_ in shape[1:]:
                n *= s_
            words = n if dt == F32 else (n + 1) // 2
            assert off_words + words <= UW, (off_words, words)
            ap = U[:, off_words:off_words + words]
            if dt != F32:
                ap = ap.bitcast(dt)
            if len(shape) == 3:
                ap = ap.rearrange("p (a b) -> p a b", a=shape[1])
            elif len(shape) == 4:
                ap = ap.rearrange("p (a b c) -> p a b c", a=shape[1], b=shape[2])
            elif len(shape) == 5:
                ap = ap.rearrange("p (a b c d) -> p a b c d", a=shape[1], b=shape[2], c=shape[3])
            return ap

        BS = carve(0, [128, 4, 8, 2, 128], BF16)
        CX = carve(4096, [128, 16, 8, 2, 32], BF16)
        Mw = carve(8192, [128, 4, 8, 128], BF16)
        Ere = carve(10240, [128, 16, 256], F32)
        Eim = carve(14336, [128, 16, 256], F32)
        CONSTS = U[:, 0:CW]
        B0 = CW

        pbank = [PS("pb%d" % i, [128, 512], F32) for i in range(8)]

        S = Sched(nc)
        A = S.add

        def MM(o, lhsT, rhs, st, sp, R, W, **kw):
            return A("pe", lambda e: e.matmul(o, lhsT=lhsT, rhs=rhs, start=st, stop=sp, **kw), reads=R, writes=W)

        def TR(o, i, idt, R, W):
            return A("pe", lambda e: e.transpose(out=o, in_=i, identity=idt), reads=R, writes=W)

        def ACT(o, i, func, R, W, bias=0.0, scale=1.0, accum=None):
            if accum is None:
                return A("act", lambda e: e.activation(out=o, in_=i, func=func, bias=bias, scale=scale), reads=R, writes=W)
            return A("act", lambda e: e.activation(out=o, in_=i, func=func, bias=bias, scale=scale, accum_out=accum), reads=R, writes=W)

        def TT(o, a, b, op, R, W, eng="dve"):
            return A(eng, lambda e: e.tensor_tensor(out=o, in0=a, in1=b, op=op), reads=R, writes=W)

        def TS(o, a, s1, s2, op0, op1, R, W, eng="dve"):
            if s2 is None and isinstance(s1, float):
                return A(eng, lambda e: e.tensor_scalar(out=o, in0=a, scalar1=s1, scalar2=0.0, op0=op0, op1=ALU.add), reads=R, writes=W)
            if s2 is None:
                return A(eng, lambda e: e.tensor_scalar(out=o, in0=a, scalar1=s1, scalar2=None, op0=op0), reads=R, writes=W)
            return A(eng, lambda e: e.tensor_scalar(out=o, in0=a, scalar1=s1, scalar2=s2, op0=op0, op1=op1), reads=R, writes=W)

        def STT(o, a, sc, b, op0, op1, R, W, eng="dve"):
            return A(eng, lambda e: e.scalar_tensor_tensor(out=o, in0=a, scalar=sc, in1=b, op0=op0, op1=op1), reads=R, writes=W)

        def CP(o, i, R, W, eng="dve"):
            return A(eng, lambda e: e.tensor_copy(out=o, in_=i), reads=R, writes=W)

        def ACP(o, i, R, W):
            return A("act", lambda e: e.activation(out=o, in_=i, func=AF.Copy), reads=R, writes=W)

        def MS(o, val, W, eng="pool"):
            return A(eng, lambda e: e.memset(o, val), writes=W)

        def RCP(o, i, R, W):
            return A("dve", lambda e: e.reciprocal(out=o, in_=i), reads=R, writes=W)

        su_ops = []

        def DMA(eng, o, i, R, W, key, slow=False):
            if key == "setup":
                key = "su%d" % len(su_ops)
                fn_ = (lambda e: e.dma_start(out=o, in_=i, allow_slow_non_contiguous=True)) if slow else (lambda e: e.dma_start(out=o, in_=i))
                op_ = A(eng, fn_, reads=R, writes=W, dma=key)
                if len(su_ops) >= 2:
                    op_.deps.append(su_ops[-2])
                su_ops.append(op_)
                return op_
            if slow:
                return A(eng, lambda e: e.dma_start(out=o, in_=i, allow_slow_non_contiguous=True), reads=R, writes=W, dma=key)
            return A(eng, lambda e: e.dma_start(out=o, in_=i), reads=R, writes=W, dma=key)

        def bc(ap, shape):
            return ap.unsqueeze(len(ap.shape)).to_broadcast(shape)

        MUL, ADD, SUB = ALU.mult, ALU.add, ALU.subtract

        so = [B0]

        def stmp(shape, dt=F32):
            ap = carve(so[0], shape, dt)
            n = 1
            for s_ in shape[1:]:
                n *= s_
            so[0] += n if dt == F32 else (n + 1) // 2
            return ap

        azf = attzT[:, :, :].rearrange("p a b -> p (a b)").bitcast(F32)
        szf = ssmzT[:, :, :].rearrange("p a b -> p (a b)").bitcast(F32)

        def big3(base, idx):
            return base[:, idx * 2048:(idx + 1) * 2048].rearrange("p (a b) -> p a b", a=16)

        ones_f = stmp([128, 128]); zer_f = stmp([128, 128]); mask_f = stmp([128, 128])
        MS(ones_f, 1.0, ["ones_f"]); MS(zer_f, 0.0, ["zer_f"])
        A("pool", lambda e: e.affine_select(out=ident_f[:], in_=ones_f, pattern=[[1, 128]], compare_op=ALU.is_equal, fill=0.0, base=0, channel_multiplier=-1), reads=["ones_f"], writes=["ident_f"])
        A("pool", lambda e: e.affine_select(out=mask_f, in_=zer_f, pattern=[[1, 128]], compare_op=ALU.is_ge, fill=-30000.0, base=0, channel_multiplier=-1), reads=["zer_f"], writes=["mask_f"])
        CP(ident_b[:], ident_f[:], ["ident_f"], ["ident_b"])
        CP(mask_b[:], mask_f, ["mask_f"], ["mask_b"])
        jm = stmp([128, 16])
        A("pool", lambda e: e.iota(jm, pattern=[[-128, 16]], base=-128, channel_multiplier=1, allow_small_or_imprecise_dtypes=True), writes=["jm"])
        for h in range(8):
            TS(bias_tab[:, h, :], jm, float(2.0 ** (-(h + 1))), None, MUL, None, ["jm"], ["bias_tab"])
        DMA("sp", g_col[:], norm_g.rearrange("(kt p) -> p kt", p=128), [], ["g_col"], "setup", slow=True)
        sg_raw = stmp([128, 1])
        DMA("sp", sg_raw, subln_g.rearrange("(p o) -> p o", o=1), [], ["sg_raw"], "setup", slow=True)
        TS(sgc[:], sg_raw, 0.4, None, MUL, None, ["sg_raw"], ["sgc"])
        bglu_raw = stmp([128, 4])
        DMA("sp", bglu_raw, b_glu.rearrange("(f p) -> p f", p=128), [], ["bglu_raw"], "setup", slow=True)
        TS(bglu_h[:], bglu_raw, 0.5, None, MUL, None, ["bglu_raw"], ["bglu_h"])
        dcol = stmp([128, 4])
        DMA("sp", dcol, s_d.rearrange("(f p) -> p f", p=128), [], ["dcol"], "setup", slow=True)
        lv = [stmp([128, 64]) for _ in range(4)]
        for t_, src in zip(lv, (lq1, lk1, lq2, lk2)):
            DMA("sp", t_, src.partition_broadcast(128), [], ["lv"], "setup")
        l12 = stmp([128, 2]); ltmp = stmp([128, 64])
        for i_ in range(2):
            TT(ltmp, lv[2 * i_], lv[2 * i_ + 1], MUL, ["lv"], ["ltmp"])
            A("dve", (lambda i_: lambda e: e.tensor_reduce(out=l12[:, i_:i_ + 1], in_=ltmp, axis=AX.X, op=ADD))(i_), reads=["ltmp"], writes=["l12"])
        ACT(l12, l12, AF.Exp, ["l12"], ["l12"])
        TT(nlam[:], l12[:, 1:2], l12[:, 0:1], SUB, ["l12"], ["nlam"])
        TS(nlam[:], nlam[:], -0.2, None, ADD, None, ["nlam"], ["nlam"])

        def sp16(name):
            return stmp([128, 16])

        lre = sp16("lre"); lim = sp16("lim"); dtt = sp16("dt")
        DMA("sp", lre, s_lre.rearrange("(pr h) p -> (h p) pr", h=2), [], ["lre"], "setup", slow=True)
        DMA("sp", lim, s_lim.rearrange("(pr h) p -> (h p) pr", h=2), [], ["lim"], "setup", slow=True)
        ldt2 = s_ldt.rearrange("(pr h) -> h pr", h=2)
        for hh in range(2):
            DMA("sp", dtt[hh * 64:(hh + 1) * 64, :], ldt2[hh].partition_broadcast(64), [], ["dtt"], "setup", slow=True)
        bre = stmp([128, 16, 16]); bim = stmp([128, 16, 16]); cre = stmp([128, 16, 16]); cim = stmp([128, 16, 16])
        for hh in range(2):
            ps_ = slice(hh * 64, (hh + 1) * 64)
            DMA("sp", bre[ps_], s_bre.rearrange("(pr h) p f -> h p pr f", h=2)[hh], [], ["bre"], "setup", slow=True)
            DMA("sp", bim[ps_], s_bim.rearrange("(pr h) p f -> h p pr f", h=2)[hh], [], ["bim"], "setup", slow=True)
        cnat_r = stmp([128, 4, 128]); cnat_i = stmp([128, 4, 128])
        for arr_, cn_, cnn in ((s_cre, cnat_r, "cnat_r"), (s_cim, cnat_i, "cnat_i")):
            src_ = arr_.rearrange("(ft gl) f p -> (gl f) ft p", ft=4)
            DMA("sp", cn_[:, :, 0:64], src_, [], [cnn], "setup")
            DMA("sp", cn_[:, :, 64:128], src_, [], [cnn], "setup")
        for ft in range(4):
            for cn_, cnn, dst_, dn_ in ((cnat_r, "cnat_r", cre, "cre"), (cnat_i, "cnat_i", cim, "cim")):
                TR(pbank[3][:, 0:128], cn_[:, ft, :], ident_f[:], [cnn, "ident_f"], ["pb3"])
                pv_ = pbank[3][:, 0:128].rearrange("p (g f) -> p g f", g=8)
                CP(dst_[0:64, 4 * ft:4 * ft + 4, :], pv_[0:64, 0:8:2, :], ["pb3"], [dn_])
                CP(dst_[64:128, 4 * ft:4 * ft + 4, :], pv_[64:128, 1:8:2, :], ["pb3"], [dn_])
        ACT(dtt, dtt, AF.Exp, ["dtt"], ["dtt"])
        TS(lre, lre, -1e-4, None, ALU.min, None, ["lre"], ["lre"])
        av = sp16("a"); th = sp16("th"); mag = sp16("mag")
        TT(av, lre, dtt, MUL, ["lre", "dtt"], ["av"])
        TT(th, lim, dtt, MUL, ["lim", "dtt"], ["th"])
        ACT(mag, av, AF.Exp, ["av"], ["mag"])
        halfpi = sp16("halfpi")
        MS(halfpi, float(math.pi / 2), ["halfpi"])
        cs = sp16("cs"); sn = sp16("sn"); t16a = sp16("t16a"); t16b = sp16("t16b"); t16c = sp16("t16c")
        ACT(sn, th, AF.Sin, ["th"], ["sn"], scale=1.0 / 32)
        TS(t16a, th, 1.0 / 32, None, MUL, None, ["th"], ["t16a"])
        TT(t16a, t16a, halfpi, ADD, ["t16a", "halfpi"], ["t16a"])
        ACT(cs, t16a, AF.Sin, ["t16a"], ["cs"])

        def csquare(re_, im_, rn, in_):
            TT(t16a, re_, re_, MUL, [rn], ["t16a"])
            TT(t16b, im_, im_, MUL, [in_], ["t16b"])
            TT(t16c, re_, im_, MUL, [rn, in_], ["t16c"])
            TT(re_, t16a, t16b, SUB, ["t16a", "t16b"], [rn])
            TS(im_, t16c, 2.0, None, MUL, None, ["t16c"], [in_])

        for _ in range(5):
            csquare(cs, sn, "cs", "sn")
        lam_re = sp16("lam_re"); lam_im = sp16("lam_im")
        TT(lam_re, mag, cs, MUL, ["mag", "cs"], ["lam_re"])
        TT(lam_im, mag, sn, MUL, ["mag", "sn"], ["lam_im"])
        num = sp16("num"); den = sp16("den"); cfr = sp16("cfr"); cfi = sp16("cfi")
        TS(num, lam_re, -1.0, None, ADD, None, ["lam_re"], ["num"])
        TT(t16a, lre, lre, MUL, ["lre"], ["t16a"])
        TT(t16b, lim, lim, MUL, ["lim"], ["t16b"])
        TT(den, t16a, t16b, ADD, ["t16a", "t16b"], ["den"])
        RCP(den, den, ["den"], ["den"])
        TT(t16a, num, lre, MUL, ["num", "lre"], ["t16a"])
        TT(t16b, lam_im, lim, MUL, ["lam_im", "lim"], ["t16b"])
        TT(cfr, t16a, t16b, ADD, ["t16a", "t16b"], ["cfr"])
        TT(cfr, cfr, den, MUL, ["cfr", "den"], ["cfr"])
        TT(t16a, lam_im, lre, MUL, ["lam_im", "lre"], ["t16a"])
        TT(t16b, num, lim, MUL, ["num", "lim"], ["t16b"])
        TT(cfi, t16a, t16b, SUB, ["t16a", "t16b"], ["cfi"])
        TT(cfi, cfi, den, MUL, ["cfi", "den"], ["cfi"])
        bbr = stmp([128, 16, 16]); bbi = stmp([128, 16, 16]); w1 = stmp([128, 16, 16]); w2 = stmp([128, 16, 16]); w3 = stmp([128, 16, 16]); w4 = stmp([128, 16, 16])
        S3 = [128, 16, 16]

        def cmul3(ore, oim, ar, ai, arn, ain, br, bi, brn, bin_, orn, oin, neg_im=False):
            TT(w1, br, bc(ar, S3), MUL, [brn, arn], ["w1"])
            TT(w2, bi, bc(ai, S3), MUL, [bin_, ain], ["w2"])
            TT(ore, w1, w2, SUB, ["w1", "w2"], [orn])
            TT(w3, bi, bc(ar, S3), MUL, [bin_, arn], ["w3"], eng="pool")
            TT(w4, br, bc(ai, S3), MUL, [brn, ain], ["w4"], eng="pool")
            TT(oim, w3, w4, ADD, ["w3", "w4"], [oin], eng="pool")
            if neg_im:
                TS(oim, oim, -1.0, None, MUL, None, [oin], [oin], eng="pool")

        cmul3(bbr, bbi, cfr, cfi, "cfr", "cfi", bre, bim, "bre", "bim", "bbr", "bbi")
        pwr = stmp([128, 16, 9]); pwi = stmp([128, 16, 9])
        MS(pwr[:, :, 0], 1.0, ["pwr"]); MS(pwi[:, :, 0], 0.0, ["pwi"])
        for k in range(1, 9):
            TT(t16a, pwr[:, :, k - 1], lam_re, MUL, ["pwr", "lam_re"], ["t16a"])
            TT(t16b, pwi[:, :, k - 1], lam_im, MUL, ["pwi", "lam_im"], ["t16b"])
            TT(pwr[:, :, k], t16a, t16b, SUB, ["t16a", "t16b"], ["pwr"])
            TT(t16a, pwr[:, :, k - 1], lam_im, MUL, ["pwr", "lam_im"], ["t16a"])
            TT(t16b, pwi[:, :, k - 1], lam_re, MUL, ["pwi", "lam_re"], ["t16b"])
            TT(pwi[:, :, k], t16a, t16b, ADD, ["t16a", "t16b"], ["pwi"])
        GPr, GPi, CPr, CPn = [hT[:, i_, :].rearrange("p (a b) -> p a b", a=16) for i_ in range(4)]
        for t_, nm in ((GPr, "GPr"), (GPi, "GPi"), (CPr, "CPr"), (CPn, "CPn")):
            MS(t_, 0.0, [nm])
        MS(CX, 0.0, ["CX"])
        Gr = stmp([128, 16, 16]); Gi = stmp([128, 16, 16])
        Gr_b = stmp([128, 16, 16]); Gi_b = stmp([128, 16, 16])
        GPr_b, GPi_b = [hT[:, i_, :].rearrange("p (a b) -> p a b", a=16) for i_ in (4, 5)]
        MS(GPr_b, 0.0, ["GPr_b"]); MS(GPi_b, 0.0, ["GPi_b"])

        def place(dst, dn, src, sn_):
            for q in range(4):
                for hh in range(2):
                    c0 = 32 * q + 16 * hh
                    ACP(dst[hh * 64:(hh + 1) * 64, q:16:4, c0:c0 + 16], src[hh * 64:(hh + 1) * 64, q:16:4, :], [sn_], [dn])

        TS(w1, cim, -1.0, None, MUL, None, ["cim"], ["w1"])
        place(CPr, "CPr", cre, "cre")
        place(CPn, "CPn", w1, "w1")
        diagD = stmp([128, 4, 128])
        for ft in range(4):
            TS(diagD[:, ft, :], ident_f[:], dcol[:, ft:ft + 1], None, MUL, None, ["ident_f", "dcol"], ["diagD"])
        for k in range(8):
            if k % 2 == 0:
                Gr_, Gi_, grn, gin_, GPr_, GPi_, gprn, gpin = Gr, Gi, "Gr", "Gi", GPr, GPi, "GPr", "GPi"
            else:
                Gr_, Gi_, grn, gin_, GPr_, GPi_, gprn, gpin = Gr_b, Gi_b, "Gr_b", "Gi_b", GPr_b, GPi_b, "GPr_b", "GPi_b"
            cmul3(Gr_, Gi_, pwr[:, :, k], pwi[:, :, k], "pwr", "pwi", bbr, bbi, "bbr", "bbi", grn, gin_)
            place(GPr_, gprn, Gr_, grn)
            place(GPi_, gpin, Gi_, gin_)
            for ft in range(4):
                for ri, (GP, gn) in enumerate(((GPr_, gprn), (GPi_, gpin))):
                    pb = pbank[ri + 4 * (k % 2)]
                    pbn = "pb%d" % (ri + 4 * (k % 2))
                    for q in range(4):
                        MM(pb[:, 0:128], GP[:, ft * 4 + q, :], ident_b[:], q == 0, q == 3, [gn, "ident_b"], [pbn])
                    ACP(BS[:, ft, 7 - k, ri, :], pb[:, 0:128], [pbn], ["BS"])
                pb = pbank[2 + 4 * (k % 2)]
                pbn = "pb%d" % (2 + 4 * (k % 2))
                for q in range(4):
                    MM(pb[:, 0:128], GPr_[:, ft * 4 + q, :], CPr[:, ft * 4 + q, :], q == 0, False, [gprn, "CPr"], [pbn])
                    MM(pb[:, 0:128], GPi_[:, ft * 4 + q, :], CPn[:, ft * 4 + q, :], False, q == 3, [gpin, "CPn"], [pbn])
                if k == 0:
                    TT(Mw[:, ft, k, :], pb[:, 0:128], diagD[:, ft, :], ADD, [pbn, "diagD"], ["Mw"])
                else:
                    ACP(Mw[:, ft, k, :], pb[:, 0:128], [pbn], ["Mw"])
        for tp in range(8):
            cmul3(Gr, Gi, pwr[:, :, tp + 1], pwi[:, :, tp + 1], "pwr", "pwi", cre, cim, "cre", "cim", "Gr", "Gi", neg_im=True)
            for ri, (G_, gn) in enumerate(((Gr, "Gr"), (Gi, "Gi"))):
                for hh in range(2):
                    ACP(CX[hh * 64:(hh + 1) * 64, :, tp, ri, 16 * hh:16 * hh + 16], G_[hh * 64:(hh + 1) * 64, :, :], [gn], ["CX"])
        Rm = sp16("Rm"); nur = sp16("nur"); nui = sp16("nui")
        TT(Rm, mag, mag, MUL, ["mag"], ["Rm"])
        TT(Rm, Rm, Rm, MUL, ["Rm"], ["Rm"])
        TT(Rm, Rm, Rm, MUL, ["Rm"], ["Rm"])
        RCP(t16a, Rm, ["Rm"], ["t16a"])
        TT(nur, pwr[:, :, 8], t16a, MUL, ["pwr", "t16a"], ["nur"])
        TT(nui, pwi[:, :, 8], t16a, MUL, ["pwi", "t16a"], ["nui"])
        TS(nui, nui, -1.0, None, MUL, None, ["nui"], ["nui"])
        MS(Ere[:, :, 0:1], 1.0, ["Ere"]); MS(Eim[:, :, 0:1], 0.0, ["Eim"])
        e1 = big3(szf, 0); e2 = big3(szf, 1); e3 = big3(azf, 0); e4 = big3(azf, 1)
        for k in range(8):
            n = 1 << k
            sh = [128, 16, n]
            TT(e1[:, :, 0:n], Ere[:, :, 0:n], bc(nur, sh), MUL, ["Ere", "nur"], ["e1"])
            TT(e2[:, :, 0:n], Eim[:, :, 0:n], bc(nui, sh), MUL, ["Eim", "nui"], ["e2"])
            TT(Ere[:, :, n:2 * n], e1[:, :, 0:n], e2[:, :, 0:n], SUB, ["e1", "e2"], ["Ere"])
            TT(e3[:, :, 0:n], Ere[:, :, 0:n], bc(nui, sh), MUL, ["Ere", "nui"], ["e3"], eng="pool")
            TT(e4[:, :, 0:n], Eim[:, :, 0:n], bc(nur, sh), MUL, ["Eim", "nur"], ["e4"], eng="pool")
            TT(Eim[:, :, n:2 * n], e3[:, :, 0:n], e4[:, :, 0:n], ADD, ["e3", "e4"], ["Eim"], eng="pool")
            if k < 7:
                csquare(nur, nui, "nur", "nui")
        Rkeep = smallf[:, 0:16]
        CP(Rkeep, Rm, ["Rm"], ["Rkeep"])
        S.barrier()
        DMA("sp", cscr, CONSTS, [], [], "cst")
        if dbg:
            S.finals.append(DMA("sp", DBG["consts"], CONSTS, [], [], "dbg"))
            S.finals.append(DMA("sp", DBG["bias_tab"], bias_tab[:, :, :].rearrange("p a b -> p (a b)"), [], [], "dbg"))
            S.finals.append(DMA("sp", DBG["misc"][:, 0:8], g_col[:], [], [], "dbg"))
            S.finals.append(DMA("sp", DBG["misc"][:, 8:9], nlam[:], [], [], "dbg", slow=True))
            S.finals.append(DMA("sp", DBG["misc"][:, 9:10], sgc[:], [], [], "dbg", slow=True))
            S.finals.append(DMA("sp", DBG["misc"][:, 10:14], bglu_h[:], [], [], "dbg"))
        S.barrier()

        class WPool:
            def __init__(self, bufs, tag):
                self.bufs = bufs; self.tag = tag; self.n = 0

            def load(self, src2d, c0, ncols, nkt):
                i = self.n % len(self.bufs)
                self.n += 1
                wb = self.bufs[i]
                nm = "%s%d" % (self.tag, i)
                srcap = src2d[:, c0:c0 + ncols].rearrange("(kt p) c -> p kt c", p=128)
                DMA("pool", wb[:, 0:nkt, 0:ncols], srcap, [], [nm], "w" + nm)
                return wb, nm

        wp_main = WPool([w_[:, :, :] for w_ in wblk], "wblk")
        wp_cur = [wp_main]

        def load_w(src2d, c0, ncols, nkt):
            return wp_cur[0].load(src2d, c0, ncols, nkt)

        def proj_fm(col0, evac):
            wb, wn = load_w(w_in, col0, 128, 8)
            for n4 in range(4):
                pb = pbank[6 + (n4 % 2)]
                pn = "pb%d" % (6 + (n4 % 2))
                for kt in range(8):
                    MM(pb[:, :], wb[:, kt, 0:128], hT[:, kt, n4 * 512:(n4 + 1) * 512], kt == 0, kt == 7, [wn, "hT"], [pn])
                evac(n4, pb, pn)

        for b in range(NSEQ):
            o_ = B0
            xt = [carve(o_ + i * 1024, [128, 1024], F32) for i in range(2)]
            xsq2 = [carve(o_ + 2048 + i * 1024, [128, 1024], F32) for i in range(2)]
            xn2 = [carve(o_ + 4096 + i * 512, [128, 1024], BF16) for i in range(2)]
            ssq = smallf[:, 16:18]
            DMA("sp", xt[0], x[b, 0:128, :], [], ["xt0"], "x0")
            for tt in range(16):
                sl_ = tt % 2
                xb = xt[sl_]; xbn = "xt%d" % sl_
                xsq = xsq2[sl_]; xn = xn2[sl_]; xnn = "xn%d" % sl_
                sq_ = smallf[:, 60 + 2 * sl_:62 + 2 * sl_]; sqn = "ssqa%d" % sl_
                if tt + 1 < 16:
                    DMA("sp", xt[1 - sl_], x[b, (tt + 1) * 128:(tt + 2) * 128, :], [], ["xt%d" % (1 - sl_)], "x%d" % (1 - sl_))
                ACT(xsq, xb, AF.Square, [xbn], ["xsq%d" % sl_, sqn], accum=sq_[:, 0:1])
                TS(sq_[:, 1:2], sq_[:, 0:1], 1.0 / D, EPS, MUL, ADD, [sqn], [sqn + "b"])
                ACT(sq_[:, 1:2], sq_[:, 1:2], AF.Sqrt, [sqn + "b"], [sqn + "b"])
                RCP(sq_[:, 1:2], sq_[:, 1:2], [sqn + "b"], [sqn + "b"])
                TS(xn, xb, sq_[:, 1:2], None, MUL, None, [xbn, sqn + "b"], [xnn])
                ptr = pbank[6 + sl_][:, :].bitcast(BF16).rearrange("p (a b) -> p a b", a=8)
                pn_ = "pb%d" % (6 + sl_)
                for kt in range(8):
                    TR(ptr[:, kt, :], xn[:, kt * 128:(kt + 1) * 128], ident_b[:], [xnn, "ident_b"], [pn_])
                TT(hT[:, :, tt * 128:(tt + 1) * 128], ptr, bc(g_col[:], [128, 8, 128]), MUL, [pn_, "g_col"], ["hT"])
            S.barrier()
            if dbg and b == 0:
                S.finals.append(DMA("sp", DBG["hT"], hT[:, :, :].rearrange("p a b -> p (a b)"), [], [], "dbg"))
                S.barrier()

            if b > 0:
                DMA("sp", CONSTS, cscr, [], [], "cst")
                S.barrier()
            o_ = CW
            uT = carve(o_, [128, L], BF16); o_ += 1024
            zs2 = carve(o_, [128, L], BF16); o_ += 1024
            rt = []
            for i in range(6):
                rt.append(carve(o_, [128, 4, 256], F32)); o_ += 1024
            Xs = carve(o_, [128, 4, 2, 258], BF16); o_ += 1032
            Rdec = carve(o_, [128, 4, 256], F32); o_ += 1024
            tmpf = carve(o_, [128, 512], F32); o_ += 512
            az = attzT[:, :, :].rearrange("p a b -> p (a b)")
            gT = az[:, 0:8192].rearrange("p (a b) -> p a b", a=4)
            ysf = az[:, 8192:12288].bitcast(F32)
            gl1 = az[:, 12288:14336].bitcast(F32)
            gl2 = az[:, 14336:16384].bitcast(F32)
            MS(Xs, 0.0, ["Xs"])
            u_sl = [(uT, "uT"), (zs2, "zs2")]
            r2 = lambda a_: a_.rearrange("p a b -> p (a b)")

            def s5_a(ft):
                ub, un = u_sl[ft % 2]

                ubd = ub.rearrange("p (s c) -> p s c", s=8)

                def ev_u(n4, pb, pn):
                    CP(ubd[:, :, n4 * 64:(n4 + 1) * 64], pb[:, :].rearrange("p (c s) -> p s c", s=8), [pn], [un])
                proj_fm(4096 + ft * 128, ev_u)

            def s5_b(ft):
                ub, un = u_sl[ft % 2]
                ubd = ub.rearrange("p (s c) -> p s c", s=8)
                for q in range(4):
                    pb = pbank[2 + q]; pn = "pb%d" % (2 + q)
                    for ri in range(2):
                        for sp_ in range(8):
                            MM(pb[:, ri * 256:(ri + 1) * 256], BS[32 * q:32 * q + 32, ft, sp_, ri, :],
                               ubd[32 * q:32 * q + 32, sp_, :], sp_ == 0, sp_ == 7, ["BS", un], [pn],
                               skip_group_check=True, tile_position=(32 * q, 0))

            def s5_c(ft):
                Er = Ere[:, ft * 4:(ft + 1) * 4, :]; Ei = Eim[:, ft * 4:(ft + 1) * 4, :]
                for q in range(4):
                    pb = pbank[2 + q]; pn = "pb%d" % (2 + q)
                    sre = pb[:, 0:256]; sim = pb[:, 256:512]
                    TT(rt[0][:, q, :], sre, Er[:, q, :], MUL, [pn], ["rt0"])
                    TT(rt[1][:, q, :], sim, Ei[:, q, :], MUL, [pn], ["rt1"])
                    TT(rt[2][:, q, :], sim, Er[:, q, :], MUL, [pn], ["rt2"])
                    TT(rt[3][:, q, :], sre, Ei[:, q, :], MUL, [pn], ["rt3"])

            def s5_de(ft):
                Er = Ere[:, ft * 4:(ft + 1) * 4, :]; Ei = Eim[:, ft * 4:(ft + 1) * 4, :]
                TT(rt[0], rt[0], rt[1], SUB, ["rt0", "rt1"], ["rt0"])
                TT(rt[2], rt[2], rt[3], ADD, ["rt2", "rt3"], ["rt2"])
                CP(Rdec, bc(Rkeep[:, ft * 4:(ft + 1) * 4], [128, 4, 256]), ["Rkeep"], ["Rdec"])
                MS(Rdec[:, :, 0:1], 0.0, ["Rdec"], eng="dve")
                A("dve", lambda e: e.tensor_tensor_scan(out=r2(rt[1]), data0=r2(Rdec), data1=r2(rt[0]), initial=0.0, op0=MUL, op1=ADD), reads=["Rdec", "rt0"], writes=["rt1"])
                A("dve", lambda e: e.tensor_tensor_scan(out=r2(rt[3]), data0=r2(Rdec), data1=r2(rt[2]), initial=0.0, op0=MUL, op1=ADD), reads=["Rdec", "rt2"], writes=["rt3"])
                TT(rt[0], rt[1], Er, MUL, ["rt1"], ["rt0"])
                TT(rt[2], rt[3], Ei, MUL, ["rt3"], ["rt2"])
                TT(Xs[:, :, 0, 1:256], rt[0][:, :, 0:255], rt[2][:, :, 0:255], ADD, ["rt0", "rt2"], ["Xs"])
                TT(rt[4], rt[3], Er, MUL, ["rt3"], ["rt4"])
                TT(rt[5], rt[1], Ei, MUL, ["rt1"], ["rt5"])
                TT(Xs[:, :, 1, 1:256], rt[4][:, :, 0:255], rt[5][:, :, 0:255], SUB, ["rt4", "rt5"], ["Xs"])

            def s5_f(ft):
                ub, un = u_sl[ft % 2]
                ubd = ub.rearrange("p (s c) -> p s c", s=8)
                for tp in range(8):
                    pb = pbank[tp % 2]; pn = "pb%d" % (tp % 2)
                    for sp_ in range(tp + 1):
                        MM(pb[:, 0:256], Mw[:, ft, tp - sp_, :], ubd[:, sp_, :], sp_ == 0, False, ["Mw", un], [pn], skip_group_check=True)
                    for q in range(4):
                        for ri in range(2):
                            MM(pb[32 * q:32 * q + 32, 0:256], CX[:, ft * 4 + q, tp, ri, :], Xs[:, q, ri, 0:256], False,
                               (q == 3 and ri == 1), ["CX", "Xs"], [pn], skip_group_check=True, tile_position=(0, 32 * q))
                    CP(ysf[:, tp:L:8], pb[:, 0:256], [pn], ["ysf"])

            def s5_g(ft):
                for hf in range(2):
                    yv = ysf[:, hf * 1024:(hf + 1) * 1024]
                    TT(gl1, yv, yv, MUL, ["ysf"], ["gl1"])
                    TS(gl1, gl1, 0.044715, 1.0, MUL, ADD, ["gl1"], ["gl1"])
                    TT(gl1, gl1, yv, MUL, ["gl1", "ysf"], ["gl1"])
                    ACT(gl2, gl1, AF.Tanh, ["gl1"], ["gl2"], scale=0.7978845608028654)
                    STT(gT[:, ft, hf * 1024:(hf + 1) * 1024], gl2, 1.0, yv, ADD, MUL, ["gl2", "ysf"], ["gT"])

            s5_a(0); s5_b(0)
            for ft in range(4):
                if ft + 1 < 4:
                    s5_a(ft + 1)
                s5_c(ft)
                if ft + 1 < 4:
                    s5_b(ft + 1)
                s5_de(ft)
                s5_f(ft)
                s5_g(ft)
            gtmp = [(tmpf, "tmpf"), (gl1[:, 0:512], "gl1"), (gl2[:, 0:512], "gl2")]
            gcnt = [0]

            def nxt_tmp():
                gcnt[0] += 1
                return gtmp[gcnt[0] % 3]

            for fo in range(4):
                def ev_z(n4, pb, pn):
                    tf_, tfn = nxt_tmp()
                    ACT(tf_, pb[:, :], AF.Tanh, [pn], [tfn], scale=0.5)
                    STT(zs2[:, n4 * 512:(n4 + 1) * 512], tf_, 1.0, pb[:, :], ADD, MUL, [tfn, pn], ["zs2"])
                proj_fm(4608 + fo * 128, ev_z)
                wb, wn = load_w(w_glu, fo * 128, 128, 4)
                for n4 in range(4):
                    pb = pbank[n4 % 2]; pn = "pb%d" % (n4 % 2)
                    for ft in range(4):
                        MM(pb[:, :], wb[:, ft, 0:128], gT[:, ft, n4 * 512:(n4 + 1) * 512], ft == 0, ft == 3, [wn, "gT"], [pn])
                    tf_, tfn = nxt_tmp()
                    ACT(tf_, pb[:, :], AF.Tanh, [pn, "bglu_h"], [tfn], bias=bglu_h[:, fo:fo + 1], scale=0.25)
                    STT(tf_, tf_, 1.0, gT[:, fo, n4 * 512:(n4 + 1) * 512], ADD, MUL, [tfn, "gT"], [tfn])
                    STT(ssmzT[:, fo, n4 * 512:(n4 + 1) * 512], tf_, 0.125, zs2[:, n4 * 512:(n4 + 1) * 512], MUL, MUL, [tfn, "zs2"], ["ssmzT"])
            S.barrier()
            if dbg and b == 0:
                S.finals.append(DMA("sp", DBG["ssmzT"], ssmzT[:, :, :].rearrange("p a b -> p (a b)"), [], [], "dbg"))
                S.finals.append(DMA("sp", DBG["gT"], az[:, 0:8192], [], [], "dbg"))
                S.barrier()

            o_ = 0
            qT = [carve(o_ + i * 1024, [128, L], BF16) for i in range(2)]; o_ += 2048
            kT = [[carve(o_ + (2 * i + c_) * 1024, [128, L], BF16) for c_ in range(2)] for i in range(2)]; o_ += 4096
            for i in range(2):
                MS(kT[i][0][64:128, :], 0.0, ["kT%d" % i])
                MS(kT[i][1][0:64, :], 0.0, ["kT%d" % i])
            zsil = [carve(o_ + i * 1024, [128, L], BF16) for i in range(2)]; o_ += 2048
            v_sb = carve(o_, [128, 16, 8, 130], BF16); o_ += 8320
            Pb = [carve(o_ + i * 256, [128, 512], BF16) for i in range(3)]; o_ += 768
            osb = [carve(o_ + i * 520, [128, 4, 130], F32) for i in range(2)]; o_ += 1040
            ofa = [carve(o_ + i * 2048, [128, 4, 4, 128], F32) for i in range(2)]; o_ += 4096
            ot_ = carve(o_, [128, 4, 128], F32); o_ += 512
            otmp = carve(o_, [128, 4, 128], F32); o_ += 512
            onb = carve(o_, [128, 4, 128], BF16); o_ += 256
            tmpa = [carve(o_ + i * 512, [128, 512], F32) for i in range(2)]; o_ += 1024
            awb = [carve(o_ + i * 1024, [128, 8, 256], BF16) for i in range(4)]; o_ += 4096
            wp_cur[0] = WPool(awb, "awb")
            rr = smallf[:, 20:28]
            ssa = [smallf[:, 28:44], smallf[:, 44:60]]
            S4 = [128, 4, 128]
            MS(v_sb[:, :, :, 128:129], 1.0, ["v_sb"])
            for vb_ in range(4):
                wb, wn = load_w(w_in, 2048 + vb_ * 256, 256, 8)
                for tt in range(16):
                    pb = pbank[6 + (tt % 2)]; pn = "pb%d" % (6 + (tt % 2))
                    for kt in range(8):
                        MM(pb[:, 0:256], hT[:, kt, tt * 128:(tt + 1) * 128], wb[:, kt, 0:256], kt == 0, kt == 7, ["hT", wn], [pn])
                    CP(v_sb[:, tt, 2 * vb_:2 * vb_ + 2, 0:128], pb[:, 0:256].rearrange("p (a b) -> p a b", a=2), [pn], ["v_sb"])

            onb2 = [onb, carve(o_, [128, 4, 128], BF16)]; o_ += 256
            ucnt = [0]

            def proj_units(h, slot):
                units = []
                for kind, col0 in (("q", h * 128), ("k", 1024 + h * 128), ("z", 3072 + h * 128)):
                    wref = {}
                    for n8 in range(4):
                        hf = ucnt[0] % 2
                        ucnt[0] += 1
                        pbh = pbank[6 + hf][:, :]; pn = "pb%d" % (6 + hf)
                        sl = slice(n8 * 512, (n8 + 1) * 512)

                        def stA(kind=kind, col0=col0, n8=n8, wref=wref):
                            if n8 == 0:
                                wref["w"] = load_w(w_in, col0, 128, 8)

                        def stB(pbh=pbh, pn=pn, sl=sl, wref=wref):
                            wb, wn = wref["w"]
                            for kt in range(8):
                                MM(pbh, wb[:, kt, 0:128], hT[:, kt, sl], kt == 0, kt == 7, [wn, "hT"], [pn])

                        def stC(kind=kind, pbh=pbh, pn=pn, sl=sl, n8=n8):
                            if kind == "q":
                                CP(qT[slot][:, sl], pbh, [pn], ["qT%d" % slot])
                            elif kind == "k":
                                CP(kT[slot][0][0:64, sl], pbh[0:64, :], [pn], ["kT%d" % slot])
                                CP(kT[slot][1][64:128, sl], pbh[64:128, :], [pn], ["kT%d" % slot])
                            else:
                                ta = tmpa[n8 % 2]; tn = "tmpa%d" % (n8 % 2)
                                ACT(ta, pbh, AF.Tanh, [pn], [tn], scale=0.5)
                                STT(zsil[slot][:, sl], ta, 1.0, pbh, ADD, MUL, [tn, pn], ["zsil%d" % slot])

                        units.append([stA, stB, stC])
                return units

            def post_units(h, par):
                sa = ssa[par]; san = "ssa%d" % par
                units = []

                def r0():
                    TS(sa, sa, 1.0 / 128, EPS, MUL, ADD, [san], [san])
                    ACT(sa, sa, AF.Sqrt, [san], [san])
                    RCP(sa, sa, [san], [san])
                units.append([r0, None, None])
                for st in range(4):
                    hf = ucnt[0] % 2
                    ucnt[0] += 1
                    ob_ = onb2[st % 2]; obn = "onb%d" % (st % 2)
                    ptr = pbank[6 + hf][:, 0:256].bitcast(BF16).rearrange("p (a b) -> p a b", a=4)
                    pn = "pb%d" % (6 + hf)

                    def stA(st=st, ob_=ob_, obn=obn):
                        TT(ob_, ofa[par][:, st], bc(sa[:, st * 4:(st + 1) * 4], S4), MUL, ["ofa%d" % par, san], [obn])

                    def stB(ob_=ob_, obn=obn, ptr=ptr, pn=pn):
                        for ib in range(4):
                            TR(ptr[:, ib, :], ob_[:, ib, :], ident_b[:], [obn, "ident_b"], [pn])

                    def stC(st=st, ptr=ptr, pn=pn):
                        STT(attzT[:, h, st * 512:(st + 1) * 512], ptr.rearrange("p a b -> p (a b)"), sgc[:, 0:1],
                            zsil[par][:, st * 512:(st + 1) * 512], MUL, MUL, [pn, "sgc", "zsil%d" % par], ["attzT"])
                    units.append([stA, stB, stC])
                return units

            class BG:
                def __init__(self, units):
                    self.units = units; self.n = 0

                def pull(self):
                    n = self.n
                    for k, stg in ((n - 2, 2), (n - 1, 1), (n, 0)):
                        if 0 <= k < len(self.units) and self.units[k][stg] is not None:
                            self.units[k][stg]()
                    self.n += 1

                def done(self):
                    return self.n >= len(self.units) + 2

                def drain(self):
                    while not self.done():
                        self.pull()

            steps = [(st, c, J) for st in range(4) for c in range(2) for J in range(4 * st + 4)]

            def emit_qk(h, i):
                st, c, J = steps[i]
                slot = h % 2
                g_ = h * len(steps) + i
                pb = pbank[g_ % 3]; pn = "pb%d" % (g_ % 3)
                jb = J - 4 * st
                c0 = max(0, jb) * 128
                pr_ = slice(c * 64, (c + 1) * 64)
                MM(pb[:, c0:512], kT[slot][c][:, J * 128:(J + 1) * 128], qT[slot][:, st * 512 + c0:(st + 1) * 512],
                   True, jb < 0, ["kT%d" % slot, "qT%d" % slot], [pn])
                if jb >= 0:
                    MM(pb[:, c0:c0 + 128], ident_b[:], mask_b[:], False, True, ["ident_b", "mask_b"], [pn])

            def emit_exp_pv(h, i):
                st, c, J = steps[i]
                par = h % 2
                g_ = h * len(steps) + i
                pb = pbank[g_ % 3]; pn = "pb%d" % (g_ % 3)
                P_ = Pb[g_ % 3]; Pn = "P%d" % (g_ % 3)
                jb = J - 4 * st
                c0 = max(0, jb) * 128
                Wh = 128 if h == 0 else (256 if h == 1 else 512)
                for s0 in range(0, 512, Wh):
                    a0 = max(s0, c0)
                    if a0 >= s0 + Wh:
                        continue
                    m = (st * 512 + s0 + Wh - 128 * J) // 128
                    ACT(P_[:, a0:s0 + Wh], pb[:, a0:s0 + Wh], AF.Exp, [pn, "bias_tab"], [Pn], bias=bias_tab[:, h, m - 1:m], scale=0.125)
                accA = pbank[3 + c]; an = "pb%d" % (3 + c)
                accB = pbank[5][:, c * 256:c * 256 + 129]; bn = "pb5"
                for ib in range(4):
                    I = 4 * st + ib
                    if I < J:
                        continue
                    if ib < 3:
                        MM(accA[:, ib * 129:(ib + 1) * 129], P_[:, ib * 128:(ib + 1) * 128], v_sb[:, J, h, 0:129],
                           (J == 0 and ib == 0), J == I, [Pn, "v_sb"], [an], skip_group_check=True)
                    else:
                        MM(accB, P_[:, ib * 128:(ib + 1) * 128], v_sb[:, J, h, 0:129],
                           J == 0, J == I, [Pn, "v_sb"], [bn], skip_group_check=True)
                if J == 4 * st + 3:
                    CP(osb[c][:, 0:3, 0:129], accA[:, 0:387].rearrange("p (a b) -> p a b", a=3), [an], ["osb%d" % c])
                    if c == 1:
                        CP(osb[0][:, 3, 0:129], pbank[5][:, 0:129], ["pb5"], ["osb0"])
                        CP(osb[1][:, 3, 0:129], pbank[5][:, 256:385], ["pb5"], ["osb1"])
                        ofn = "ofa%d" % par
                        ofs = ofa[par][:, st]
                        ssl = ssa[par][:, st * 4:(st + 1) * 4]

                        def pc1(ofs=ofs, ofn=ofn):
                            RCP(rr[:, 0:4], osb[0][:, :, 128], ["osb0"], ["rr0"])
                            RCP(rr[:, 4:8], osb[1][:, :, 128], ["osb1"], ["rr1"])
                            TS(rr[:, 4:8], rr[:, 4:8], nlam[:, 0:1], None, MUL, None, ["rr1", "nlam"], ["rr1"])
                            TT(ofs, osb[0][:, :, 0:128], bc(rr[:, 0:4], S4), MUL, ["osb0", "rr0"], [ofn])

                        def pc2(ofs=ofs, ofn=ofn):
                            TT(ot_, osb[1][:, :, 0:128], bc(rr[:, 4:8], S4), MUL, ["osb1", "rr1"], ["ot"])
                            TT(ofs, ofs, ot_, ADD, [ofn, "ot"], [ofn])

                        def pc3(ofs=ofs, ofn=ofn, ssl=ssl, par=par):
                            TT(otmp, ofs, ofs, MUL, [ofn], ["otmp"])
                            A("dve", lambda e: e.tensor_reduce(out=ssl, in_=otmp, axis=AX.X, op=ADD), reads=["otmp"], writes=["ssa%d" % par])
                        dq.extend([pc1, pc2, pc3])

            import collections
            dq = collections.deque()
            BG(proj_units(0, 0)).drain()
            NS = len(steps)
            LOOK = 2
            bg = BG([])
            for g in range(8 * NS):
                h, i = divmod(g, NS)
                if i == 0:
                    while dq:
                        dq.popleft()()
                    bg.drain()
                    us = []
                    if h > 0:
                        us += post_units(h - 1, (h - 1) % 2)
                    if h < 7:
                        us += proj_units(h + 1, (h + 1) % 2)
                    bg = BG(us)
                if g == 0:
                    for la in range(LOOK):
                        emit_qk(0, la)
                gl = g + LOOK
                if gl < 8 * NS:
                    hl, il = divmod(gl, NS)
                    if il < LOOK and hl > h:
                        bg.drain()
                    emit_qk(hl, il)
                emit_exp_pv(h, i)
                if i % 2 == 1:
                    bg.pull()
                elif dq:
                    dq.popleft()()
            while dq:
                dq.popleft()()
            bg.drain()
            BG(post_units(7, 1)).drain()
            wp_cur[0] = wp_main
            S.barrier()
            if dbg and b == 0:
                S.finals.append(DMA("sp", DBG["attzT"], attzT[:, :, :].rearrange("p a b -> p (a b)"), [], [], "dbg"))
                S.finals.append(DMA("sp", DBG["small"], smallf[:], [], [], "dbg"))
                S.barrier()

            o_ = 0
            mergedT = carve(o_, [128, 8, L], BF16); o_ += 8192
            gaf = carve(o_, [128, L], F32); o_ += 2048
            gsf = carve(o_, [128, L], F32); o_ += 2048
            m1f = carve(o_, [128, L], F32); o_ += 2048
            m2 = carve(o_, [128, 512], F32); o_ += 512
            xo2 = [carve(o_ + i * 1024, [128, 1024], F32) for i in range(2)]; o_ += 2048
            xr2 = [carve(o_ + i * 1024, [128, 1024], F32) for i in range(2)]; o_ += 2048
            ob2 = [carve(o_ + i * 1024, [128, 1024], F32) for i in range(2)]; o_ += 2048
            fgb = carve(o_, [128, 1024], F32); o_ += 1024
            wo_sb = carve(o_, [128, 8, 1024], BF16); o_ += 4096
            owb = [carve(o_ + i * 512, [128, 8, 128], BF16) for i in range(6)]; o_ += 3072
            wp_cur[0] = WPool(owb, "owb")
            DMA("sp", fgb, final_g.partition_broadcast(128), [], ["fgb"], "fg")
            for dt_ in range(8):
                for (col0, dstf, dn) in ((5120, gaf, "gaf"), (6144, gsf, "gsf")):
                    def ev_g(n4, pb, pn, dstf=dstf, dn=dn):
                        ACT(dstf[:, n4 * 512:(n4 + 1) * 512], pb[:, :], AF.Tanh, [pn], [dn], scale=0.5)
                    proj_fm(col0 + dt_ * 128, ev_g)
                wa, wan = load_w(w_o_att, dt_ * 128, 128, 8)
                for n4 in range(4):
                    tk = slice(n4 * 512, (n4 + 1) * 512)
                    pb = pbank[n4 % 2]; pn = "pb%d" % (n4 % 2)
                    for hh in range(8):
                        MM(pb[:, :], wa[:, hh, 0:128], attzT[:, hh, tk], hh == 0, hh == 7, [wan, "attzT"], [pn])
                    STT(m1f[:, tk], gaf[:, tk], 1.0, pb[:, :], ADD, MUL, ["gaf", pn], ["m1f"])
                ws, wsn = load_w(w_o_ssm, dt_ * 128, 128, 4)
                for n4 in range(4):
                    tk = slice(n4 * 512, (n4 + 1) * 512)
                    pb = pbank[2 + n4 % 2]; pn = "pb%d" % (2 + n4 % 2)
                    for ft in range(4):
                        MM(pb[:, :], ws[:, ft, 0:128], ssmzT[:, ft, tk], ft == 0, ft == 3, [wsn, "ssmzT"], [pn])
                    STT(m2, gsf[:, tk], 1.0, pb[:, :], ADD, MUL, ["gsf", pn], ["m2"])
                    TT(mergedT[:, dt_, tk], m1f[:, tk], m2, ADD, ["m1f", "m2"], ["mergedT"])
            if dbg and b == 0:
                S.barrier()
                S.finals.append(DMA("sp", DBG["mergedT"], mergedT.rearrange("p a b -> p (a b)"), [], [], "dbg"))
                S.barrier()
            for cb in range(4):
                DMA("pool", wo_sb[:, :, cb * 256:(cb + 1) * 256], w_out[:, cb * 256:(cb + 1) * 256].rearrange("(kt p) c -> p kt c", p=128), [], ["wo_sb"], "wo")
            DMA("sp", xr2[0], x[b, 0:128, :], [], ["xr0"], "xr0")
            for tt in range(16):
                sl_ = tt % 2
                xr = xr2[sl_]; xo = xo2[sl_]; ob = ob2[sl_]
                xrn = "xr%d" % sl_; xon = "xo%d" % sl_; obn = "ob%d" % sl_; sqn = "ssqf%d" % sl_
                sq_ = smallf[:, 60 + 2 * sl_:62 + 2 * sl_]
                if tt + 1 < 16:
                    DMA("sp", xr2[1 - sl_], x[b, (tt + 1) * 128:(tt + 2) * 128, :], [], ["xr%d" % (1 - sl_)], "xr%d" % (1 - sl_))
                for hf in range(2):
                    pb = pbank[4 + hf + 2 * sl_]; pn = "pb%d" % (4 + hf + 2 * sl_)
                    for dt_ in range(8):
                        MM(pb[:, :], mergedT[:, dt_, tt * 128:(tt + 1) * 128], wo_sb[:, dt_, hf * 512:(hf + 1) * 512], dt_ == 0, dt_ == 7, ["mergedT", "wo_sb"], [pn])
                    STT(xo[:, hf * 512:(hf + 1) * 512], pb[:, :], 0.5, xr[:, hf * 512:(hf + 1) * 512], MUL, ADD, [pn, xrn], [xon])
                ACT(ob, xo, AF.Square, [xon], [obn, sqn], accum=sq_[:, 0:1])
                TS(sq_[:, 1:2], sq_[:, 0:1], 1.0 / D, EPS, MUL, ADD, [sqn], [sqn + "b"])
                ACT(sq_[:, 1:2], sq_[:, 1:2], AF.Sqrt, [sqn + "b"], [sqn + "b"])
                RCP(sq_[:, 1:2], sq_[:, 1:2], [sqn + "b"], [sqn + "b"])
                STT(ob, xo, sq_[:, 1:2], fgb, MUL, MUL, [xon, sqn + "b", "fgb", obn], [obn])
                fin = DMA("sp", out[b, tt * 128:(tt + 1) * 128, :], ob, [obn], [], "out")
                S.finals.append(fin)
            wp_cur[0] = wp_main
            S.barrier()
        S.emit()
    return nc


_CACHE = {}


def kernel(**inputs):
    if "nc" not in _CACHE:
        _CACHE["nc"] = build_program()
    nc = _CACHE["nc"]
    f = lambda a: np.ascontiguousarray(np.asarray(a, dtype=np.float32))
    x = f(inputs["x"])
    common = {
        "norm_g": f(inputs["norm_g"]).reshape(D),
        "w_in": f(inputs["w_in"]).reshape(D, 7168),
        "lambda_q1": f(inputs["lambda_q1"]).reshape(64), "lambda_k1": f(inputs["lambda_k1"]).reshape(64),
        "lambda_q2": f(inputs["lambda_q2"]).reshape(64), "lambda_k2": f(inputs["lambda_k2"]).reshape(64),
        "subln_g": f(inputs["subln_g"]).reshape(128),
        "w_o_att": f(inputs["w_o_att"]).reshape(D, D),
        "ssm_lambda_re": f(inputs["ssm_lambda_re"]).reshape(32, 64), "ssm_lambda_im": f(inputs["ssm_lambda_im"]).reshape(32, 64),
        "ssm_log_dt": f(inputs["ssm_log_dt"]).reshape(32),
        "ssm_b_re": f(inputs["ssm_b_re"]).reshape(32, 64, 16), "ssm_b_im": f(inputs["ssm_b_im"]).reshape(32, 64, 16),
        "ssm_c_re": f(inputs["ssm_c_re"]).reshape(32, 16, 64), "ssm_c_im": f(inputs["ssm_c_im"]).reshape(32, 16, 64),
        "ssm_d": f(inputs["ssm_d"]).reshape(512),
        "w_glu": f(inputs["w_glu"]).reshape(512, 512), "b_glu": f(inputs["b_glu"]).reshape(512),
        "w_o_ssm": f(inputs["w_o_ssm"]).reshape(512, D), "w_out": f(inputs["w_out"]).reshape(D, D),
        "final_g": f(inputs["final_g"]).reshape(D),
    }
    in_maps = []
    for c in range(8):
        m = dict(common)
        m["x"] = np.ascontiguousarray(x[2 * c:2 * c + 2])
        in_maps.append(m)
    res = run_bass_kernel_spmd(nc, in_maps, core_ids=list(range(8)))
    return np.concatenate([np.asarray(r["out"]) for r in res.results], axis=0).astype(np.float32)
```

```python
import contextlib
import math
import numpy as np
import concourse.bass as bass
import concourse.mybir as mybir
from concourse.bass_utils import run_bass_kernel_spmd

F32 = mybir.dt.float32
BF16 = mybir.dt.bfloat16
ALU = mybir.AluOpType
AF = mybir.ActivationFunctionType
AX = mybir.AxisListType

ENGS = ("pe", "act", "dve", "pool", "sp")
SEM_CAP = 1800
L = 2048
D = 1024
NSEQ = 2
EPS = 1e-5


class Op:
    __slots__ = ("eng", "fn", "deps", "signal", "sem", "val", "is_dma", "dkey", "idx")

    def __init__(self, eng, fn, is_dma, dkey):
        self.eng = eng
        self.fn = fn
        self.deps = []
        self.signal = is_dma
        self.sem = None
        self.val = 0
        self.is_dma = is_dma
        self.dkey = dkey


class Sched:
    def __init__(self, nc):
        self.nc = nc
        self.ops = {e: [] for e in ENGS}
        self.last_w = {}
        self.readers = {}
        self.all_ops = []
        self.bar_deps = {e: [] for e in ENGS}
        self.pending_dma = []
        self.finals = []

    def barrier(self):
        deps = []
        for e in ENGS:
            for op in reversed(self.ops[e]):
                if not op.is_dma:
                    deps.append(op)
                    break
        last_per_key = {}
        for op in self.pending_dma:
            last_per_key[op.dkey] = op
        deps.extend(last_per_key.values())
        self.pending_dma = []
        for e in ENGS:
            self.bar_deps[e] = list(deps)
        self.last_w = {}
        self.readers = {}

    def add(self, eng, fn, reads=(), writes=(), dma=None):
        op = Op(eng, fn, dma is not None, dma)
        deps = list(self.bar_deps[eng])
        self.bar_deps[eng] = []
        for r in reads:
            w = self.last_w.get(r)
            if w is not None:
                deps.append(w)
        for r in writes:
            w = self.last_w.get(r)
            if w is not None:
                deps.append(w)
            deps.extend(self.readers.get(r, ()))
        op.idx = len(self.all_ops)
        seen = set()
        latest = {}
        keep = []
        for d in deps:
            if id(d) in seen or d is op:
                continue
            seen.add(id(d))
            if d.eng == "pe" and eng == "pe" and not d.is_dma and dma is None:
                continue
            if not d.is_dma and d.eng in ("pe", "act", "dve"):
                if d.eng not in latest or d.idx > latest[d.eng].idx:
                    latest[d.eng] = d
                continue
            keep.append(d)
        keep.extend(latest.values())
        for d in keep:
            op.deps.append(d)
            d.signal = True
        for r in reads:
            self.readers.setdefault(r, []).append(op)
        for r in writes:
            self.last_w[r] = op
            self.readers[r] = []
        self.ops[eng].append(op)
        self.all_ops.append(op)
        if op.is_dma:
            self.pending_dma.append(op)
        return op

    def emit(self):
        nc = self.nc
        sem_specs = []
        cur = {}
        for op in self.all_ops:
            if not op.signal:
                continue
            key = ("dma", op.dkey) if op.is_dma else ("eng", op.eng)
            if key not in cur or cur[key][1] >= SEM_CAP:
                sem_specs.append("s%d_%s" % (len(sem_specs), str(key[1])[:12]))
                cur[key] = [len(sem_specs) - 1, 0]
            cur[key][1] += 1
            op.sem = cur[key][0]
            op.val = cur[key][1] * (16 if op.is_dma else 1)
        with contextlib.ExitStack() as es:
            sems = [es.enter_context(nc.semaphore(n)) for n in sem_specs]
            block = es.enter_context(nc.Block())
            finals = self.finals

            def run(eng_name):
                def body(eng):
                    waited = {}

                    def w(d):
                        if waited.get(d.sem, 0) >= d.val:
                            return
                        eng.wait_ge(sems[d.sem], d.val)
                        waited[d.sem] = d.val

                    for op in self.ops[eng_name]:
                        for d in op.deps:
                            w(d)
                        ins = op.fn(eng)
                        if op.signal:
                            ins.then_inc(sems[op.sem], 16 if op.is_dma else 1)
                    if eng_name == "sp":
                        fmax = {}
                        for d in finals:
                            if d.sem not in fmax or d.val > fmax[d.sem].val:
                                fmax[d.sem] = d
                        for d in fmax.values():
                            w(d)

                return body

            block.tensor(run("pe"))
            block.scalar(run("act"))
            block.vector(run("dve"))
            block.gpsimd(run("pool"))
            block.sync(run("sp"))
        return len(sem_specs)


def build_program(dbg=False):
    nc = bass.Bass("TRN2", target_bir_lowering=False)
    DBG = {}

    def din(name, shape):
        return nc.dram_tensor(name, shape, F32, kind="ExternalInput").ap()

    x = din("x", [NSEQ, L, D])
    norm_g = din("norm_g", [D])
    w_in = din("w_in", [D, 7168])
    lq1 = din("lambda_q1", [64]); lk1 = din("lambda_k1", [64])
    lq2 = din("lambda_q2", [64]); lk2 = din("lambda_k2", [64])
    subln_g = din("subln_g", [128])
    w_o_att = din("w_o_att", [D, D])
    s_lre = din("ssm_lambda_re", [32, 64]); s_lim = din("ssm_lambda_im", [32, 64])
    s_ldt = din("ssm_log_dt", [32])
    s_bre = din("ssm_b_re", [32, 64, 16]); s_bim = din("ssm_b_im", [32, 64, 16])
    s_cre = din("ssm_c_re", [32, 16, 64]); s_cim = din("ssm_c_im", [32, 16, 64])
    s_d = din("ssm_d", [512])
    w_glu = din("w_glu", [512, 512]); b_glu = din("b_glu", [512])
    w_o_ssm = din("w_o_ssm", [512, D]); w_out = din("w_out", [D, D])
    final_g = din("final_g", [D])
    out = nc.dram_tensor("out", [NSEQ, L, D], F32, kind="ExternalOutput").ap()
    CW = 18432
    cscr = nc.dram_tensor("cscr", [128, CW], F32, kind="Internal").ap()
    if dbg:
        DBG["consts"] = nc.dram_tensor("d_consts", [128, CW], F32, kind="ExternalOutput").ap()
        DBG["hT"] = nc.dram_tensor("d_hT", [128, 8 * L], BF16, kind="ExternalOutput").ap()
        DBG["ssmzT"] = nc.dram_tensor("d_ssmzT", [128, 4 * L], BF16, kind="ExternalOutput").ap()
        DBG["gT"] = nc.dram_tensor("d_gT", [128, 4 * L], BF16, kind="ExternalOutput").ap()
        DBG["attzT"] = nc.dram_tensor("d_attzT", [128, 8 * L], BF16, kind="ExternalOutput").ap()
        DBG["mergedT"] = nc.dram_tensor("d_mergedT", [128, 8 * L], BF16, kind="ExternalOutput").ap()
        DBG["small"] = nc.dram_tensor("d_small", [128, 64], F32, kind="ExternalOutput").ap()
        DBG["bias_tab"] = nc.dram_tensor("d_bias_tab", [128, 128], F32, kind="ExternalOutput").ap()
        DBG["misc"] = nc.dram_tensor("d_misc", [128, 16], F32, kind="ExternalOutput").ap()

    es = contextlib.ExitStack()
    with es:
        def T(name, shape, dt):
            return es.enter_context(nc.sbuf_tensor(name, shape, dt))

        def PS(name, shape, dt):
            return es.enter_context(nc.psum_tensor(name, shape, dt))

        hT = T("hT", [128, 8, L], BF16)
        attzT = T("attzT", [128, 8, L], BF16)
        ssmzT = T("ssmzT", [128, 4, L], BF16)
        wblk = [T("wblk%d" % i, [128, 8, 256], BF16) for i in range(2)]
        ident_f = T("ident_f", [128, 128], F32)
        ident_b = T("ident_b", [128, 128], BF16)
        mask_b = T("mask_b", [128, 128], BF16)
        bias_tab = T("bias_tab", [128, 8, 16], F32)
        g_col = T("g_col", [128, 8], F32)
        sgc = T("sgc", [128, 1], F32)
        nlam = T("nlam", [128, 1], F32)
        bglu_h = T("bglu_h", [128, 4], F32)
        smallf = T("smallf", [128, 64], F32)
        UW = 29440
        U = T("U", [128, UW], F32)

        def carve(off_words, shape, dt):
            n = 1
            for s_ in shape[1:]:
                n *= s_
            words = n if dt == F32 else (n + 1) // 2
            assert off_words + words <= UW, (off_words, words)
            ap = U[:, off_words:off_words + words]
            if dt != F32:
                ap = ap.bitcast(dt)
            if len(shape) == 3:
                ap = ap.rearrange("p (a b) -> p a b", a=shape[1])
            elif len(shape) == 4:
                ap = ap.rearrange("p (a b c) -> p a b c", a=shape[1], b=shape[2])
            elif len(shape) == 5:
                ap = ap.rearrange("p (a b c d) -> p a b c d", a=shape[1], b=shape[2], c=shape[3])
            return ap

        BS = carve(0, [128, 4, 8, 2, 128], BF16)
        CX = carve(4096, [128, 16, 8, 2, 32], BF16)
        Mw = carve(8192, [128, 4, 8, 128], BF16)
        Ere = carve(10240, [128, 16, 256], F32)
        Eim = carve(14336, [128, 16, 256], F32)
        CONSTS = U[:, 0:CW]
        B0 = CW

        pbank = [PS("pb%d" % i, [128, 512], F32) for i in range(8)]

        S = Sched(nc)
        A = S.add

        def MM(o, lhsT, rhs, st, sp, R, W, **kw):
            return A("pe", lambda e: e.matmul(o, lhsT=lhsT, rhs=rhs, start=st, stop=sp, **kw), reads=R, writes=W)

        def TR(o, i, idt, R, W):
            return A("pe", lambda e: e.transpose(out=o, in_=i, identity=idt), reads=R, writes=W)

        def ACT(o, i, func, R, W, bias=0.0, scale=1.0, accum=None):
            if accum is None:
                return A("act", lambda e: e.activation(out=o, in_=i, func=func, bias=bias, scale=scale), reads=R, writes=W)
            return A("act", lambda e: e.activation(out=o, in_=i, func=func, bias=bias, scale=scale, accum_out=accum), reads=R, writes=W)

        def TT(o, a, b, op, R, W, eng="dve"):
            return A(eng, lambda e: e.tensor_tensor(out=o, in0=a, in1=b, op=op), reads=R, writes=W)

        def TS(o, a, s1, s2, op0, op1, R, W, eng="dve"):
            if s2 is None and isinstance(s1, float):
                return A(eng, lambda e: e.tensor_scalar(out=o, in0=a, scalar1=s1, scalar2=0.0, op0=op0, op1=ALU.add), reads=R, writes=W)
            if s2 is None:
                return A(eng, lambda e: e.tensor_scalar(out=o, in0=a, scalar1=s1, scalar2=None, op0=op0), reads=R, writes=W)
            return A(eng, lambda e: e.tensor_scalar(out=o, in0=a, scalar1=s1, scalar2=s2, op0=op0, op1=op1), reads=R, writes=W)

        def STT(o, a, sc, b, op0, op1, R, W, eng="dve"):
            return A(eng, lambda e: e.scalar_tensor_tensor(out=o, in0=a, scalar=sc, in1=b, op0=op0, op1=op1), reads=R, writes=W)

        def CP(o, i, R, W, eng="dve"):
            return A(eng, lambda e: e.tensor_copy(out=o, in_=i), reads=R, writes=W)

        def ACP(o, i, R, W):
            return A("act", lambda e: e.activation(out=o, in_=i, func=AF.Copy), reads=R, writes=W)

        def MS(o, val, W, eng="pool"):
            return A(eng, lambda e: e.memset(o, val), writes=W)

        def RCP(o, i, R, W):
            return A("dve", lambda e: e.reciprocal(out=o, in_=i), reads=R, writes=W)

        su_ops = []

        def DMA(eng, o, i, R, W, key, slow=False):
            if key == "setup":
                key = "su%d" % len(su_ops)
                fn_ = (lambda e: e.dma_start(out=o, in_=i, allow_slow_non_contiguous=True)) if slow else (lambda e: e.dma_start(out=o, in_=i))
                op_ = A(eng, fn_, reads=R, writes=W, dma=key)
                if len(su_ops) >= 2:
                    op_.deps.append(su_ops[-2])
                su_ops.append(op_)
                return op_
            if slow:
                return A(eng, lambda e: e.dma_start(out=o, in_=i, allow_slow_non_contiguous=True), reads=R, writes=W, dma=key)
            return A(eng, lambda e: e.dma_start(out=o, in_=i), reads=R, writes=W, dma=key)

        def bc(ap, shape):
            return ap.unsqueeze(len(ap.shape)).to_broadcast(shape)

        MUL, ADD, SUB = ALU.mult, ALU.add, ALU.subtract

        so = [B0]

        def stmp(shape, dt=F32):
            ap = carve(so[0], shape, dt)
            n = 1
            for s_ in shape[1:]:
                n *= s_
            so[0] += n if dt == F32 else (n + 1) // 2
            return ap

        azf = attzT[:, :, :].rearrange("p a b -> p (a b)").bitcast(F32)
        szf = ssmzT[:, :, :].rearrange("p a b -> p (a b)").bitcast(F32)

        def big3(base, idx):
            return base[:, idx * 2048:(idx + 1) * 2048].rearrange("p (a b) -> p a b", a=16)

        ones_f = stmp([128, 128]); zer_f = stmp([128, 128]); mask_f = stmp([128, 128])
        MS(ones_f, 1.0, ["ones_f"]); MS(zer_f, 0.0, ["zer_f"])
        A("pool", lambda e: e.affine_select(out=ident_f[:], in_=ones_f, pattern=[[1, 128]], compare_op=ALU.is_equal, fill=0.0, base=0, channel_multiplier=-1), reads=["ones_f"], writes=["ident_f"])
        A("pool", lambda e: e.affine_select(out=mask_f, in_=zer_f, pattern=[[1, 128]], compare_op=ALU.is_ge, fill=-30000.0, base=0, channel_multiplier=-1), reads=["zer_f"], writes=["mask_f"])
        CP(ident_b[:], ident_f[:], ["ident_f"], ["ident_b"])
        CP(mask_b[:], mask_f, ["mask_f"], ["mask_b"])
        jm = stmp([128, 16])
        A("pool", lambda e: e.iota(jm, pattern=[[-128, 16]], base=-128, channel_multiplier=1, allow_small_or_imprecise_dtypes=True), writes=["jm"])
        for h in range(8):
            TS(bias_tab[:, h, :], jm, float(2.0 ** (-(h + 1))), None, MUL, None, ["jm"], ["bias_tab"])
        DMA("sp", g_col[:], norm_g.rearrange("(kt p) -> p kt", p=128), [], ["g_col"], "setup", slow=True)
        sg_raw = stmp([128, 1])
        DMA("sp", sg_raw, subln_g.rearrange("(p o) -> p o", o=1), [], ["sg_raw"], "setup", slow=True)
        TS(sgc[:], sg_raw, 0.4, None, MUL, None, ["sg_raw"], ["sgc"])
        bglu_raw = stmp([128, 4])
        DMA("sp", bglu_raw, b_glu.rearrange("(f p) -> p f", p=128), [], ["bglu_raw"], "setup", slow=True)
        TS(bglu_h[:], bglu_raw, 0.5, None, MUL, None, ["bglu_raw"], ["bglu_h"])
        dcol = stmp([128, 4])
        DMA("sp", dcol, s_d.rearrange("(f p) -> p f", p=128), [], ["dcol"], "setup", slow=True)
        lv = [stmp([128, 64]) for _ in range(4)]
        for t_, src in zip(lv, (lq1, lk1, lq2, lk2)):
            DMA("sp", t_, src.partition_broadcast(128), [], ["lv"], "setup")
        l12 = stmp([128, 2]); ltmp = stmp([128, 64])
        for i_ in range(2):
            TT(ltmp, lv[2 * i_], lv[2 * i_ + 1], MUL, ["lv"], ["ltmp"])
            A("dve", (lambda i_: lambda e: e.tensor_reduce(out=l12[:, i_:i_ + 1], in_=ltmp, axis=AX.X, op=ADD))(i_), reads=["ltmp"], writes=["l12"])
        ACT(l12, l12, AF.Exp, ["l12"], ["l12"])
        TT(nlam[:], l12[:, 1:2], l12[:, 0:1], SUB, ["l12"], ["nlam"])
        TS(nlam[:], nlam[:], -0.2, None, ADD, None, ["nlam"], ["nlam"])

        def sp16(name):
            return stmp([128, 16])

        lre = sp16("lre"); lim = sp16("lim"); dtt = sp16("dt")
        DMA("sp", lre, s_lre.rearrange("(pr h) p -> (h p) pr", h=2), [], ["lre"], "setup", slow=True)
        DMA("sp", lim, s_lim.rearrange("(pr h) p -> (h p) pr", h=2), [], ["lim"], "setup", slow=True)
        ldt2 = s_ldt.rearrange("(pr h) -> h pr", h=2)
        for hh in range(2):
            DMA("sp", dtt[hh * 64:(hh + 1) * 64, :], ldt2[hh].partition_broadcast(64), [], ["dtt"], "setup", slow=True)
        bre = stmp([128, 16, 16]); bim = stmp([128, 16, 16]); cre = stmp([128, 16, 16]); cim = stmp([128, 16, 16])
        for hh in range(2):
            ps_ = slice(hh * 64, (hh + 1) * 64)
            DMA("sp", bre[ps_], s_bre.rearrange("(pr h) p f -> h p pr f", h=2)[hh], [], ["bre"], "setup", slow=True)
            DMA("sp", bim[ps_], s_bim.rearrange("(pr h) p f -> h p pr f", h=2)[hh], [], ["bim"], "setup", slow=True)
        cnat_r = stmp([128, 4, 128]); cnat_i = stmp([128, 4, 128])
        for arr_, cn_, cnn in ((s_cre, cnat_r, "cnat_r"), (s_cim, cnat_i, "cnat_i")):
            src_ = arr_.rearrange("(ft gl) f p -> (gl f) ft p", ft=4)
            DMA("sp", cn_[:, :, 0:64], src_, [], [cnn], "setup")
            DMA("sp", cn_[:, :, 64:128], src_, [], [cnn], "setup")
        for ft in range(4):
            for cn_, cnn, dst_, dn_ in ((cnat_r, "cnat_r", cre, "cre"), (cnat_i, "cnat_i", cim, "cim")):
                TR(pbank[3][:, 0:128], cn_[:, ft, :], ident_f[:], [cnn, "ident_f"], ["pb3"])
                pv_ = pbank[3][:, 0:128].rearrange("p (g f) -> p g f", g=8)
                CP(dst_[0:64, 4 * ft:4 * ft + 4, :], pv_[0:64, 0:8:2, :], ["pb3"], [dn_])
                CP(dst_[64:128, 4 * ft:4 * ft + 4, :], pv_[64:128, 1:8:2, :], ["pb3"], [dn_])
        ACT(dtt, dtt, AF.Exp, ["dtt"], ["dtt"])
        TS(lre, lre, -1e-4, None, ALU.min, None, ["lre"], ["lre"])
        av = sp16("a"); th = sp16("th"); mag = sp16("mag")
        TT(av, lre, dtt, MUL, ["lre", "dtt"], ["av"])
        TT(th, lim, dtt, MUL, ["lim", "dtt"], ["th"])
        ACT(mag, av, AF.Exp, ["av"], ["mag"])
        halfpi = sp16("halfpi")
        MS(halfpi, float(math.pi / 2), ["halfpi"])
        cs = sp16("cs"); sn = sp16("sn"); t16a = sp16("t16a"); t16b = sp16("t16b"); t16c = sp16("t16c")
        ACT(sn, th, AF.Sin, ["th"], ["sn"], scale=1.0 / 32)
        TS(t16a, th, 1.0 / 32, None, MUL, None, ["th"], ["t16a"])
        TT(t16a, t16a, halfpi, ADD, ["t16a", "halfpi"], ["t16a"])
        ACT(cs, t16a, AF.Sin, ["t16a"], ["cs"])

        def csquare(re_, im_, rn, in_):
            TT(t16a, re_, re_, MUL, [rn], ["t16a"])
            TT(t16b, im_, im_, MUL, [in_], ["t16b"])
            TT(t16c, re_, im_, MUL, [rn, in_], ["t16c"])
            TT(re_, t16a, t16b, SUB, ["t16a", "t16b"], [rn])
            TS(im_, t16c, 2.0, None, MUL, None, ["t16c"], [in_])

        for _ in range(5):
            csquare(cs, sn, "cs", "sn")
        lam_re = sp16("lam_re"); lam_im = sp16("lam_im")
        TT(lam_re, mag, cs, MUL, ["mag", "cs"], ["lam_re"])
        TT(lam_im, mag, sn, MUL, ["mag", "sn"], ["lam_im"])
        num = sp16("num"); den = sp16("den"); cfr = sp16("cfr"); cfi = sp16("cfi")
        TS(num, lam_re, -1.0, None, ADD, None, ["lam_re"], ["num"])
        TT(t16a, lre, lre, MUL, ["lre"], ["t16a"])
        TT(t16b, lim, lim, MUL, ["lim"], ["t16b"])
        TT(den, t16a, t16b, ADD, ["t16a", "t16b"], ["den"])
        RCP(den, den, ["den"], ["den"])
        TT(t16a, num, lre, MUL, ["num", "lre"], ["t16a"])
        TT(t16b, lam_im, lim, MUL, ["lam_im", "lim"], ["t16b"])
        TT(cfr, t16a, t16b, ADD, ["t16a", "t16b"], ["cfr"])
        TT(cfr, cfr, den, MUL, ["cfr", "den"], ["cfr"])
        TT(t16a, lam_im, lre, MUL, ["lam_im", "lre"], ["t16a"])
        TT(t16b, num, lim, MUL, ["num", "lim"], ["t16b"])
        TT(cfi, t16a, t16b, SUB, ["t16a", "t16b"], ["cfi"])
        TT(cfi, cfi, den, MUL, ["cfi", "den"], ["cfi"])
        bbr = stmp([128, 16, 16]); bbi = stmp([128, 16, 16]); w1 = stmp([128, 16, 16]); w2 = stmp([128, 16, 16]); w3 = stmp([128, 16, 16]); w4 = stmp([128, 16, 16])
        S3 = [128, 16, 16]

        def cmul3(ore, oim, ar, ai, arn, ain, br, bi, brn, bin_, orn, oin, neg_im=False):
            TT(w1, br, bc(ar, S3), MUL, [brn, arn], ["w1"])
            TT(w2, bi, bc(ai, S3), MUL, [bin_, ain], ["w2"])
            TT(ore, w1, w2, SUB, ["w1", "w2"], [orn])
            TT(w3, bi, bc(ar, S3), MUL, [bin_, arn], ["w3"], eng="pool")
            TT(w4, br, bc(ai, S3), MUL, [brn, ain], ["w4"], eng="pool")
            TT(oim, w3, w4, ADD, ["w3", "w4"], [oin], eng="pool")
            if neg_im:
                TS(oim, oim, -1.0, None, MUL, None, [oin], [oin], eng="pool")

        cmul3(bbr, bbi, cfr, cfi, "cfr", "cfi", bre, bim, "bre", "bim", "bbr", "bbi")
        pwr = stmp([128, 16, 9]); pwi = stmp([128, 16, 9])
        MS(pwr[:, :, 0], 1.0, ["pwr"]); MS(pwi[:, :, 0], 0.0, ["pwi"])
        for k in range(1, 9):
            TT(t16a, pwr[:, :, k - 1], lam_re, MUL, ["pwr", "lam_re"], ["t16a"])
            TT(t16b, pwi[:, :, k - 1], lam_im, MUL, ["pwi", "lam_im"], ["t16b"])
            TT(pwr[:, :, k], t16a, t16b, SUB, ["t16a", "t16b"], ["pwr"])
            TT(t16a, pwr[:, :, k - 1], lam_im, MUL, ["pwr", "lam_im"], ["t16a"])
            TT(t16b, pwi[:, :, k - 1], lam_re, MUL, ["pwi", "lam_re"], ["t16b"])
            TT(pwi[:, :, k], t16a, t16b, ADD, ["t16a", "t16b"], ["pwi"])
        GPr, GPi, CPr, CPn = [hT[:, i_, :].rearrange("p (a b) -> p a b", a=16) for i_ in range(4)]
        for t_, nm in ((GPr, "GPr"), (GPi, "GPi"), (CPr, "CPr"), (CPn, "CPn")):
            MS(t_, 0.0, [nm])
        MS(CX, 0.0, ["CX"])
        Gr = stmp([128, 16, 16]); Gi = stmp([128, 16, 16])
        Gr_b = stmp([128, 16, 16]); Gi_b = stmp([128, 16, 16])
        GPr_b, GPi_b = [hT[:, i_, :].rearrange("p (a b) -> p a b", a=16) for i_ in (4, 5)]
        MS(GPr_b, 0.0, ["GPr_b"]); MS(GPi_b, 0.0, ["GPi_b"])

        def place(dst, dn, src, sn_):
            dps = dst.ap[0][0]; sps = src.ap[0][0]
            for hh in range(2):
                d_ap = bass.AP(dst.tensor, dst.offset + hh * 64 * dps + 16 * hh, [[dps, 64], [512, 4], [160, 4], [1, 16]])
                s_ap = bass.AP(src.tensor, src.offset + hh * 64 * sps, [[sps, 64], [64, 4], [16, 4], [1, 16]])
                ACP(d_ap, s_ap, [sn_], [dn])

        TS(w1, cim, -1.0, None, MUL, None, ["cim"], ["w1"])
        place(CPr, "CPr", cre, "cre")
        place(CPn, "CPn", w1, "w1")
        diagD = stmp([128, 4, 128])
        for ft in range(4):
            TS(diagD[:, ft, :], ident_f[:], dcol[:, ft:ft + 1], None, MUL, None, ["ident_f", "dcol"], ["diagD"])
        for k in range(8):
            if k % 2 == 0:
                Gr_, Gi_, grn, gin_, GPr_, GPi_, gprn, gpin = Gr, Gi, "Gr", "Gi", GPr, GPi, "GPr", "GPi"
            else:
                Gr_, Gi_, grn, gin_, GPr_, GPi_, gprn, gpin = Gr_b, Gi_b, "Gr_b", "Gi_b", GPr_b, GPi_b, "GPr_b", "GPi_b"
            cmul3(Gr_, Gi_, pwr[:, :, k], pwi[:, :, k], "pwr", "pwi", bbr, bbi, "bbr", "bbi", grn, gin_)
            place(GPr_, gprn, Gr_, grn)
            place(GPi_, gpin, Gi_, gin_)
            for ri, (GP, gn) in enumerate(((GPr_, gprn), (GPi_, gpin))):
                pb = pbank[ri + 4 * (k % 2)]
                pbn = "pb%d" % (ri + 4 * (k % 2))
                for ft in range(4):
                    for q in range(4):
                        MM(pb[:, ft * 128:(ft + 1) * 128], GP[:, ft * 4 + q, :], ident_b[:], q == 0, q == 3, [gn, "ident_b"], [pbn], skip_group_check=True)
                ACP(BS[:, :, 7 - k, ri, :], pb[:, :].rearrange("p (f c) -> p f c", f=4), [pbn], ["BS"])
            pb = pbank[2 + 4 * (k % 2)]
            pbn = "pb%d" % (2 + 4 * (k % 2))
            for ft in range(4):
                for q in range(4):
                    MM(pb[:, ft * 128:(ft + 1) * 128], GPr_[:, ft * 4 + q, :], CPr[:, ft * 4 + q, :], q == 0, False, [gprn, "CPr"], [pbn], skip_group_check=True)
                    MM(pb[:, ft * 128:(ft + 1) * 128], GPi_[:, ft * 4 + q, :], CPn[:, ft * 4 + q, :], False, q == 3, [gpin, "CPn"], [pbn], skip_group_check=True)
            pb3 = pb[:, :].rearrange("p (f c) -> p f c", f=4)
            if k == 0:
                TT(Mw[:, :, k, :], pb3, diagD, ADD, [pbn, "diagD"], ["Mw"])
            else:
                ACP(Mw[:, :, k, :], pb3, [pbn], ["Mw"])
        for tp in range(8):
            cmul3(Gr, Gi, pwr[:, :, tp + 1], pwi[:, :, tp + 1], "pwr", "pwi", cre, cim, "cre", "cim", "Gr", "Gi", neg_im=True)
            for ri, (G_, gn) in enumerate(((Gr, "Gr"), (Gi, "Gi"))):
                for hh in range(2):
                    ACP(CX[hh * 64:(hh + 1) * 64, :, tp, ri, 16 * hh:16 * hh + 16], G_[hh * 64:(hh + 1) * 64, :, :], [gn], ["CX"])
        Rm = sp16("Rm"); nur = sp16("nur"); nui = sp16("nui")
        TT(Rm, mag, mag, MUL, ["mag"], ["Rm"])
        TT(Rm, Rm, Rm, MUL, ["Rm"], ["Rm"])
        TT(Rm, Rm, Rm, MUL, ["Rm"], ["Rm"])
        RCP(t16a, Rm, ["Rm"], ["t16a"])
        TT(nur, pwr[:, :, 8], t16a, MUL, ["pwr", "t16a"], ["nur"])
        TT(nui, pwi[:, :, 8], t16a, MUL, ["pwi", "t16a"], ["nui"])
        TS(nui, nui, -1.0, None, MUL, None, ["nui"], ["nui"])
        MS(Ere[:, :, 0:1], 1.0, ["Ere"]); MS(Eim[:, :, 0:1], 0.0, ["Eim"])
        e1 = big3(szf, 0); e2 = big3(szf, 1); e3 = big3(azf, 0); e4 = big3(azf, 1)
        for k in range(8):
            n = 1 << k
            sh = [128, 16, n]
            TT(e1[:, :, 0:n], Ere[:, :, 0:n], bc(nur, sh), MUL, ["Ere", "nur"], ["e1"])
            TT(e2[:, :, 0:n], Eim[:, :, 0:n], bc(nui, sh), MUL, ["Eim", "nui"], ["e2"])
            TT(Ere[:, :, n:2 * n], e1[:, :, 0:n], e2[:, :, 0:n], SUB, ["e1", "e2"], ["Ere"])
            TT(e3[:, :, 0:n], Ere[:, :, 0:n], bc(nui, sh), MUL, ["Ere", "nui"], ["e3"], eng="pool")
            TT(e4[:, :, 0:n], Eim[:, :, 0:n], bc(nur, sh), MUL, ["Eim", "nur"], ["e4"], eng="pool")
            TT(Eim[:, :, n:2 * n], e3[:, :, 0:n], e4[:, :, 0:n], ADD, ["e3", "e4"], ["Eim"], eng="pool")
            if k < 7:
                csquare(nur, nui, "nur", "nui")
        Rkeep = smallf[:, 0:16]
        CP(Rkeep, Rm, ["Rm"], ["Rkeep"])
        S.barrier()
        DMA("sp", cscr, CONSTS, [], [], "cst")
        if dbg:
            S.finals.append(DMA("sp", DBG["consts"], CONSTS, [], [], "dbg"))
            S.finals.append(DMA("sp", DBG["bias_tab"], bias_tab[:, :, :].rearrange("p a b -> p (a b)"), [], [], "dbg"))
            S.finals.append(DMA("sp", DBG["misc"][:, 0:8], g_col[:], [], [], "dbg"))
            S.finals.append(DMA("sp", DBG["misc"][:, 8:9], nlam[:], [], [], "dbg", slow=True))
            S.finals.append(DMA("sp", DBG["misc"][:, 9:10], sgc[:], [], [], "dbg", slow=True))
            S.finals.append(DMA("sp", DBG["misc"][:, 10:14], bglu_h[:], [], [], "dbg"))
        S.barrier()

        class WPool:
            def __init__(self, bufs, tag):
                self.bufs = bufs; self.tag = tag; self.n = 0

            def load(self, src2d, c0, ncols, nkt):
                i = self.n % len(self.bufs)
                self.n += 1
                wb = self.bufs[i]
                nm = "%s%d" % (self.tag, i)
                srcap = src2d[:, c0:c0 + ncols].rearrange("(kt p) c -> p kt c", p=128)
                DMA("pool", wb[:, 0:nkt, 0:ncols], srcap, [], [nm], "w" + nm)
                return wb, nm

        wp_main = WPool([w_[:, :, :] for w_ in wblk], "wblk")
        wp_cur = [wp_main]

        def load_w(src2d, c0, ncols, nkt):
            return wp_cur[0].load(src2d, c0, ncols, nkt)

        def proj_fm(col0, evac):
            wb, wn = load_w(w_in, col0, 128, 8)
            for n4 in range(4):
                pb = pbank[6 + (n4 % 2)]
                pn = "pb%d" % (6 + (n4 % 2))
                for kt in range(8):
                    MM(pb[:, :], wb[:, kt, 0:128], hT[:, kt, n4 * 512:(n4 + 1) * 512], kt == 0, kt == 7, [wn, "hT"], [pn])
                evac(n4, pb, pn)

        for b in range(NSEQ):
            o_ = B0
            xt = [carve(o_ + i * 1024, [128, 1024], F32) for i in range(2)]
            xsq2 = [carve(o_ + 2048 + i * 1024, [128, 1024], F32) for i in range(2)]
            xn2 = [carve(o_ + 4096 + i * 512, [128, 1024], BF16) for i in range(2)]
            ssq = smallf[:, 16:18]
            DMA("sp", xt[0], x[b, 0:128, :], [], ["xt0"], "x0")
            for tt in range(16):
                sl_ = tt % 2
                xb = xt[sl_]; xbn = "xt%d" % sl_
                xsq = xsq2[sl_]; xn = xn2[sl_]; xnn = "xn%d" % sl_
                sq_ = smallf[:, 60 + 2 * sl_:62 + 2 * sl_]; sqn = "ssqa%d" % sl_
                if tt + 1 < 16:
                    DMA("sp", xt[1 - sl_], x[b, (tt + 1) * 128:(tt + 2) * 128, :], [], ["xt%d" % (1 - sl_)], "x%d" % (1 - sl_))
                ACT(xsq, xb, AF.Square, [xbn], ["xsq%d" % sl_, sqn], accum=sq_[:, 0:1])
                TS(sq_[:, 1:2], sq_[:, 0:1], 1.0 / D, EPS, MUL, ADD, [sqn], [sqn + "b"])
                ACT(sq_[:, 1:2], sq_[:, 1:2], AF.Sqrt, [sqn + "b"], [sqn + "b"])
                RCP(sq_[:, 1:2], sq_[:, 1:2], [sqn + "b"], [sqn + "b"])
                TS(xn, xb, sq_[:, 1:2], None, MUL, None, [xbn, sqn + "b"], [xnn])
                ptr = pbank[6 + sl_][:, :].bitcast(BF16).rearrange("p (a b) -> p a b", a=8)
                pn_ = "pb%d" % (6 + sl_)
                for kt in range(8):
                    TR(ptr[:, kt, :], xn[:, kt * 128:(kt + 1) * 128], ident_b[:], [xnn, "ident_b"], [pn_])
                TT(hT[:, :, tt * 128:(tt + 1) * 128], ptr, bc(g_col[:], [128, 8, 128]), MUL, [pn_, "g_col"], ["hT"])
            S.barrier()
            if dbg and b == 0:
                S.finals.append(DMA("sp", DBG["hT"], hT[:, :, :].rearrange("p a b -> p (a b)"), [], [], "dbg"))
                S.barrier()

            if b > 0:
                DMA("sp", CONSTS, cscr, [], [], "cst")
                S.barrier()
            o_ = CW
            uT = carve(o_, [128, L], BF16); o_ += 1024
            zs2 = carve(o_, [128, L], BF16); o_ += 1024
            rt = []
            for i in range(6):
                rt.append(carve(o_, [128, 4, 256], F32)); o_ += 1024
            Xs = carve(o_, [128, 4, 2, 258], BF16); o_ += 1032
            Rdec = carve(o_, [128, 4, 256], F32); o_ += 1024
            tmpf = carve(o_, [128, 512], F32); o_ += 512
            az = attzT[:, :, :].rearrange("p a b -> p (a b)")
            gT = az[:, 0:8192].rearrange("p (a b) -> p a b", a=4)
            ysf = az[:, 8192:12288].bitcast(F32)
            gl1 = az[:, 12288:14336].bitcast(F32)
            gl2 = az[:, 14336:16384].bitcast(F32)
            MS(Xs, 0.0, ["Xs"])
            u_sl = [(uT, "uT"), (zs2, "zs2")]
            r2 = lambda a_: a_.rearrange("p a b -> p (a b)")

            def s5_a(ft):
                ub, un = u_sl[ft % 2]

                ubd = ub.rearrange("p (s c) -> p s c", s=8)

                def ev_u(n4, pb, pn):
                    CP(ubd[:, :, n4 * 64:(n4 + 1) * 64], pb[:, :].rearrange("p (c s) -> p s c", s=8), [pn], [un])
                proj_fm(4096 + ft * 128, ev_u)

            def s5_b(ft):
                ub, un = u_sl[ft % 2]
                ubd = ub.rearrange("p (s c) -> p s c", s=8)
                for q in range(4):
                    pb = pbank[2 + q]; pn = "pb%d" % (2 + q)
                    for ri in range(2):
                        for sp_ in range(8):
                            MM(pb[:, ri * 256:(ri + 1) * 256], BS[32 * q:32 * q + 32, ft, sp_, ri, :],
                               ubd[32 * q:32 * q + 32, sp_, :], sp_ == 0, sp_ == 7, ["BS", un], [pn],
                               skip_group_check=True, tile_position=(32 * q, 0))

            def s5_c(ft):
                Er = Ere[:, ft * 4:(ft + 1) * 4, :]; Ei = Eim[:, ft * 4:(ft + 1) * 4, :]
                for q in range(4):
                    pb = pbank[2 + q]; pn = "pb%d" % (2 + q)
                    sre = pb[:, 0:256]; sim = pb[:, 256:512]
                    TT(rt[0][:, q, :], sre, Er[:, q, :], MUL, [pn], ["rt0"])
                    TT(rt[1][:, q, :], sim, Ei[:, q, :], MUL, [pn], ["rt1"])
                    TT(rt[2][:, q, :], sim, Er[:, q, :], MUL, [pn], ["rt2"])
                    TT(rt[3][:, q, :], sre, Ei[:, q, :], MUL, [pn], ["rt3"])

            def s5_de(ft):
                Er = Ere[:, ft * 4:(ft + 1) * 4, :]; Ei = Eim[:, ft * 4:(ft + 1) * 4, :]
                TT(rt[0], rt[0], rt[1], SUB, ["rt0", "rt1"], ["rt0"])
                TT(rt[2], rt[2], rt[3], ADD, ["rt2", "rt3"], ["rt2"])
                CP(Rdec, bc(Rkeep[:, ft * 4:(ft + 1) * 4], [128, 4, 256]), ["Rkeep"], ["Rdec"])
                MS(Rdec[:, :, 0:1], 0.0, ["Rdec"], eng="dve")
                A("dve", lambda e: e.tensor_tensor_scan(out=r2(rt[1]), data0=r2(Rdec), data1=r2(rt[0]), initial=0.0, op0=MUL, op1=ADD), reads=["Rdec", "rt0"], writes=["rt1"])
                A("dve", lambda e: e.tensor_tensor_scan(out=r2(rt[3]), data0=r2(Rdec), data1=r2(rt[2]), initial=0.0, op0=MUL, op1=ADD), reads=["Rdec", "rt2"], writes=["rt3"])
                TT(rt[0], rt[1], Er, MUL, ["rt1"], ["rt0"])
                TT(rt[2], rt[3], Ei, MUL, ["rt3"], ["rt2"])
                TT(Xs[:, :, 0, 1:256], rt[0][:, :, 0:255], rt[2][:, :, 0:255], ADD, ["rt0", "rt2"], ["Xs"])
                TT(rt[4], rt[3], Er, MUL, ["rt3"], ["rt4"])
                TT(rt[5], rt[1], Ei, MUL, ["rt1"], ["rt5"])
                TT(Xs[:, :, 1, 1:256], rt[4][:, :, 0:255], rt[5][:, :, 0:255], SUB, ["rt4", "rt5"], ["Xs"])

            def s5_f(ft):
                ub, un = u_sl[ft % 2]
                ubd = ub.rearrange("p (s c) -> p s c", s=8)
                for tp in range(8):
                    pb = pbank[tp % 2]; pn = "pb%d" % (tp % 2)
                    for sp_ in range(tp + 1):
                        MM(pb[:, 0:256], Mw[:, ft, tp - sp_, :], ubd[:, sp_, :], sp_ == 0, False, ["Mw", un], [pn], skip_group_check=True)
                    for q in range(4):
                        for ri in range(2):
                            MM(pb[32 * q:32 * q + 32, 0:256], CX[:, ft * 4 + q, tp, ri, :], Xs[:, q, ri, 0:256], False,
                               (q == 3 and ri == 1), ["CX", "Xs"], [pn], skip_group_check=True, tile_position=(0, 32 * q))
                    CP(ysf[:, tp:L:8], pb[:, 0:256], [pn], ["ysf"])

            def s5_g(ft):
                for hf in range(2):
                    yv = ysf[:, hf * 1024:(hf + 1) * 1024]
                    TT(gl1, yv, yv, MUL, ["ysf"], ["gl1"])
                    TS(gl1, gl1, 0.044715, 1.0, MUL, ADD, ["gl1"], ["gl1"])
                    TT(gl1, gl1, yv, MUL, ["gl1", "ysf"], ["gl1"])
                    ACT(gl2, gl1, AF.Tanh, ["gl1"], ["gl2"], scale=0.7978845608028654)
                    STT(gT[:, ft, hf * 1024:(hf + 1) * 1024], gl2, 1.0, yv, ADD, MUL, ["gl2", "ysf"], ["gT"])

            s5_a(0); s5_b(0)
            for ft in range(4):
                if ft + 1 < 4:
                    s5_a(ft + 1)
                s5_c(ft)
                if ft + 1 < 4:
                    s5_b(ft + 1)
                s5_de(ft)
                s5_f(ft)
                s5_g(ft)
            gtmp = [(tmpf, "tmpf"), (gl1[:, 0:512], "gl1"), (gl2[:, 0:512], "gl2")]
            gcnt = [0]

            def nxt_tmp():
                gcnt[0] += 1
                return gtmp[gcnt[0] % 3]

            for fo in range(4):
                def ev_z(n4, pb, pn):
                    tf_, tfn = nxt_tmp()
                    ACT(tf_, pb[:, :], AF.Tanh, [pn], [tfn], scale=0.5)
                    STT(zs2[:, n4 * 512:(n4 + 1) * 512], tf_, 1.0, pb[:, :], ADD, MUL, [tfn, pn], ["zs2"])
                proj_fm(4608 + fo * 128, ev_z)
                wb, wn = load_w(w_glu, fo * 128, 128, 4)
                for n4 in range(4):
                    pb = pbank[n4 % 2]; pn = "pb%d" % (n4 % 2)
                    for ft in range(4):
                        MM(pb[:, :], wb[:, ft, 0:128], gT[:, ft, n4 * 512:(n4 + 1) * 512], ft == 0, ft == 3, [wn, "gT"], [pn])
                    tf_, tfn = nxt_tmp()
                    ACT(tf_, pb[:, :], AF.Tanh, [pn, "bglu_h"], [tfn], bias=bglu_h[:, fo:fo + 1], scale=0.25)
                    STT(tf_, tf_, 1.0, gT[:, fo, n4 * 512:(n4 + 1) * 512], ADD, MUL, [tfn, "gT"], [tfn])
                    STT(ssmzT[:, fo, n4 * 512:(n4 + 1) * 512], tf_, 0.125, zs2[:, n4 * 512:(n4 + 1) * 512], MUL, MUL, [tfn, "zs2"], ["ssmzT"])
            S.barrier()
            if dbg and b == 0:
                S.finals.append(DMA("sp", DBG["ssmzT"], ssmzT[:, :, :].rearrange("p a b -> p (a b)"), [], [], "dbg"))
                S.finals.append(DMA("sp", DBG["gT"], az[:, 0:8192], [], [], "dbg"))
                S.barrier()

            o_ = 0
            qT = [carve(o_ + i * 1024, [128, L], BF16) for i in range(2)]; o_ += 2048
            kT = [[carve(o_ + (2 * i + c_) * 1024, [128, L], BF16) for c_ in range(2)] for i in range(2)]; o_ += 4096
            for i in range(2):
                MS(kT[i][0][64:128, :], 0.0, ["kT%d" % i])
                MS(kT[i][1][0:64, :], 0.0, ["kT%d" % i])
            zsil = [carve(o_ + i * 1024, [128, L], BF16) for i in range(2)]; o_ += 2048
            v_sb = carve(o_, [128, 16, 8, 130], BF16); o_ += 8320
            Pb = [carve(o_ + i * 256, [128, 512], BF16) for i in range(3)]; o_ += 768
            osb = [carve(o_ + i * 520, [128, 4, 130], F32) for i in range(2)]; o_ += 1040
            ofa = [carve(o_ + i * 2048, [128, 4, 4, 128], F32) for i in range(2)]; o_ += 4096
            ot_ = carve(o_, [128, 4, 128], F32); o_ += 512
            otmp = carve(o_, [128, 4, 128], F32); o_ += 512
            onb = carve(o_, [128, 4, 128], BF16); o_ += 256
            tmpa = [carve(o_ + i * 512, [128, 512], F32) for i in range(2)]; o_ += 1024
            awb = [carve(o_ + i * 1024, [128, 8, 256], BF16) for i in range(4)]; o_ += 4096
            wp_cur[0] = WPool(awb, "awb")
            rr = smallf[:, 20:28]
            ssa = [smallf[:, 28:44], smallf[:, 44:60]]
            S4 = [128, 4, 128]
            MS(v_sb[:, :, :, 128:129], 1.0, ["v_sb"])
            for vb_ in range(4):
                wb, wn = load_w(w_in, 2048 + vb_ * 256, 256, 8)
                for tt in range(16):
                    pb = pbank[6 + (tt % 2)]; pn = "pb%d" % (6 + (tt % 2))
                    for kt in range(8):
                        MM(pb[:, 0:256], hT[:, kt, tt * 128:(tt + 1) * 128], wb[:, kt, 0:256], kt == 0, kt == 7, ["hT", wn], [pn])
                    CP(v_sb[:, tt, 2 * vb_:2 * vb_ + 2, 0:128], pb[:, 0:256].rearrange("p (a b) -> p a b", a=2), [pn], ["v_sb"])

            onb2 = [onb, carve(o_, [128, 4, 128], BF16)]; o_ += 256
            ucnt = [0]

            def proj_units(h, slot):
                units = []
                for kind, col0 in (("q", h * 128), ("k", 1024 + h * 128), ("z", 3072 + h * 128)):
                    wref = {}
                    for n8 in range(4):
                        hf = ucnt[0] % 2
                        ucnt[0] += 1
                        pbh = pbank[6 + hf][:, :]; pn = "pb%d" % (6 + hf)
                        sl = slice(n8 * 512, (n8 + 1) * 512)

                        def stA(kind=kind, col0=col0, n8=n8, wref=wref):
                            if n8 == 0:
                                wref["w"] = load_w(w_in, col0, 128, 8)

                        def stB(pbh=pbh, pn=pn, sl=sl, wref=wref):
                            wb, wn = wref["w"]
                            for kt in range(8):
                                MM(pbh, wb[:, kt, 0:128], hT[:, kt, sl], kt == 0, kt == 7, [wn, "hT"], [pn])

                        def stC(kind=kind, pbh=pbh, pn=pn, sl=sl, n8=n8):
                            if kind == "q":
                                CP(qT[slot][:, sl], pbh, [pn], ["qT%d" % slot])
                            elif kind == "k":
                                CP(kT[slot][0][0:64, sl], pbh[0:64, :], [pn], ["kT%d" % slot])
                                CP(kT[slot][1][64:128, sl], pbh[64:128, :], [pn], ["kT%d" % slot])
                            else:
                                ta = tmpa[n8 % 2]; tn = "tmpa%d" % (n8 % 2)
                                ACT(ta, pbh, AF.Tanh, [pn], [tn], scale=0.5)
                                STT(zsil[slot][:, sl], ta, 1.0, pbh, ADD, MUL, [tn, pn], ["zsil%d" % slot])

                        units.append([stA, stB, stC])
                return units

            def post_units(h, par):
                sa = ssa[par]; san = "ssa%d" % par
                units = []

                def r0():
                    TS(sa, sa, 1.0 / 128, EPS, MUL, ADD, [san], [san])
                    ACT(sa, sa, AF.Sqrt, [san], [san])
                    RCP(sa, sa, [san], [san])
                units.append([r0, None, None])
                for st in range(4):
                    hf = ucnt[0] % 2
                    ucnt[0] += 1
                    ob_ = onb2[st % 2]; obn = "onb%d" % (st % 2)
                    ptr = pbank[6 + hf][:, 0:256].bitcast(BF16).rearrange("p (a b) -> p a b", a=4)
                    pn = "pb%d" % (6 + hf)

                    def stA(st=st, ob_=ob_, obn=obn):
                        TT(ob_, ofa[par][:, st], bc(sa[:, st * 4:(st + 1) * 4], S4), MUL, ["ofa%d" % par, san], [obn])

                    def stB(ob_=ob_, obn=obn, ptr=ptr, pn=pn):
                        for ib in range(4):
                            TR(ptr[:, ib, :], ob_[:, ib, :], ident_b[:], [obn, "ident_b"], [pn])

                    def stC(st=st, ptr=ptr, pn=pn):
                        STT(attzT[:, h, st * 512:(st + 1) * 512], ptr.rearrange("p a b -> p (a b)"), sgc[:, 0:1],
                            zsil[par][:, st * 512:(st + 1) * 512], MUL, MUL, [pn, "sgc", "zsil%d" % par], ["attzT"])
                    units.append([stA, stB, stC])
                return units

            class BG:
                def __init__(self, units):
                    self.units = units; self.n = 0

                def pull(self):
                    n = self.n
                    for k, stg in ((n - 2, 2), (n - 1, 1), (n, 0)):
                        if 0 <= k < len(self.units) and self.units[k][stg] is not None:
                            self.units[k][stg]()
                    self.n += 1

                def done(self):
                    return self.n >= len(self.units) + 2

                def drain(self):
                    while not self.done():
                        self.pull()

            steps = [(st, c, J) for st in range(4) for c in range(2) for J in range(4 * st + 4)]

            def emit_qk(h, i):
                st, c, J = steps[i]
                slot = h % 2
                g_ = h * len(steps) + i
                pb = pbank[g_ % 3]; pn = "pb%d" % (g_ % 3)
                jb = J - 4 * st
                c0 = max(0, jb) * 128
                pr_ = slice(c * 64, (c + 1) * 64)
                MM(pb[:, c0:512], kT[slot][c][:, J * 128:(J + 1) * 128], qT[slot][:, st * 512 + c0:(st + 1) * 512],
                   True, jb < 0, ["kT%d" % slot, "qT%d" % slot], [pn])
                if jb >= 0:
                    MM(pb[:, c0:c0 + 128], ident_b[:], mask_b[:], False, True, ["ident_b", "mask_b"], [pn])

            def emit_exp_pv(h, i):
                st, c, J = steps[i]
                par = h % 2
                g_ = h * len(steps) + i
                pb = pbank[g_ % 3]; pn = "pb%d" % (g_ % 3)
                P_ = Pb[g_ % 3]; Pn = "P%d" % (g_ % 3)
                jb = J - 4 * st
                c0 = max(0, jb) * 128
                Wh = 128 if h == 0 else (256 if h == 1 else 512)
                for s0 in range(0, 512, Wh):
                    a0 = max(s0, c0)
                    if a0 >= s0 + Wh:
                        continue
                    m = (st * 512 + s0 + Wh - 128 * J) // 128
                    ACT(P_[:, a0:s0 + Wh], pb[:, a0:s0 + Wh], AF.Exp, [pn, "bias_tab"], [Pn], bias=bias_tab[:, h, m - 1:m], scale=0.125)
                accA = pbank[3 + c]; an = "pb%d" % (3 + c)
                accB = pbank[5][:, c * 256:c * 256 + 129]; bn = "pb5"
                for ib in range(4):
                    I = 4 * st + ib
                    if I < J:
                        continue
                    if ib < 3:
                        MM(accA[:, ib * 129:(ib + 1) * 129], P_[:, ib * 128:(ib + 1) * 128], v_sb[:, J, h, 0:129],
                           (J == 0 and ib == 0), J == I, [Pn, "v_sb"], [an], skip_group_check=True)
                    else:
                        MM(accB, P_[:, ib * 128:(ib + 1) * 128], v_sb[:, J, h, 0:129],
                           J == 0, J == I, [Pn, "v_sb"], [bn], skip_group_check=True)
                if J == 4 * st + 3:
                    CP(osb[c][:, 0:3, 0:129], accA[:, 0:387].rearrange("p (a b) -> p a b", a=3), [an], ["osb%d" % c])
                    if c == 1:
                        CP(osb[0][:, 3, 0:129], pbank[5][:, 0:129], ["pb5"], ["osb0"])
                        CP(osb[1][:, 3, 0:129], pbank[5][:, 256:385], ["pb5"], ["osb1"])
                        ofn = "ofa%d" % par
                        ofs = ofa[par][:, st]
                        ssl = ssa[par][:, st * 4:(st + 1) * 4]

                        def pc1(ofs=ofs, ofn=ofn):
                            RCP(rr[:, 0:4], osb[0][:, :, 128], ["osb0"], ["rr0"])
                            RCP(rr[:, 4:8], osb[1][:, :, 128], ["osb1"], ["rr1"])
                            TS(rr[:, 4:8], rr[:, 4:8], nlam[:, 0:1], None, MUL, None, ["rr1", "nlam"], ["rr1"])
                            TT(ofs, osb[0][:, :, 0:128], bc(rr[:, 0:4], S4), MUL, ["osb0", "rr0"], [ofn])

                        def pc2(ofs=ofs, ofn=ofn):
                            TT(ot_, osb[1][:, :, 0:128], bc(rr[:, 4:8], S4), MUL, ["osb1", "rr1"], ["ot"])
                            TT(ofs, ofs, ot_, ADD, [ofn, "ot"], [ofn])

                        def pc3(ofs=ofs, ofn=ofn, ssl=ssl, par=par):
                            TT(otmp, ofs, ofs, MUL, [ofn], ["otmp"])
                            A("dve", lambda e: e.tensor_reduce(out=ssl, in_=otmp, axis=AX.X, op=ADD), reads=["otmp"], writes=["ssa%d" % par])
                        dq.extend([pc1, pc2, pc3])

            import collections
            dq = collections.deque()
            BG(proj_units(0, 0)).drain()
            NS = len(steps)
            LOOK = 2
            bg = BG([])
            for g in range(8 * NS):
                h, i = divmod(g, NS)
                if i == 0:
                    while dq:
                        dq.popleft()()
                    bg.drain()
                    us = []
                    if h > 0:
                        us += post_units(h - 1, (h - 1) % 2)
                    if h < 7:
                        us += proj_units(h + 1, (h + 1) % 2)
                    bg = BG(us)
                if g == 0:
                    for la in range(LOOK):
                        emit_qk(0, la)
                gl = g + LOOK
                if gl < 8 * NS:
                    hl, il = divmod(gl, NS)
                    if il < LOOK and hl > h:
                        bg.drain()
                    emit_qk(hl, il)
                emit_exp_pv(h, i)
                if i % 2 == 1:
                    bg.pull()
                elif dq:
                    dq.popleft()()
            while dq:
                dq.popleft()()
            bg.drain()
            BG(post_units(7, 1)).drain()
            wp_cur[0] = wp_main
            S.barrier()
            if dbg and b == 0:
                S.finals.append(DMA("sp", DBG["attzT"], attzT[:, :, :].rearrange("p a b -> p (a b)"), [], [], "dbg"))
                S.finals.append(DMA("sp", DBG["small"], smallf[:], [], [], "dbg"))
                S.barrier()

            o_ = 0
            mergedT = carve(o_, [128, 8, L], BF16); o_ += 8192
            gaf = carve(o_, [128, L], F32); o_ += 2048
            gsf = carve(o_, [128, L], F32); o_ += 2048
            m1f = carve(o_, [128, L], F32); o_ += 2048
            m2 = carve(o_, [128, 512], F32); o_ += 512
            xo2 = [carve(o_ + i * 1024, [128, 1024], F32) for i in range(2)]; o_ += 2048
            xr2 = [carve(o_ + i * 1024, [128, 1024], F32) for i in range(2)]; o_ += 2048
            ob2 = [carve(o_ + i * 1024, [128, 1024], F32) for i in range(2)]; o_ += 2048
            fgb = carve(o_, [128, 1024], F32); o_ += 1024
            wo_sb = carve(o_, [128, 8, 1024], BF16); o_ += 4096
            owb = [carve(o_ + i * 512, [128, 8, 128], BF16) for i in range(6)]; o_ += 3072
            wp_cur[0] = WPool(owb, "owb")
            DMA("sp", fgb, final_g.partition_broadcast(128), [], ["fgb"], "fg")
            for dt_ in range(8):
                for (col0, dstf, dn) in ((5120, gaf, "gaf"), (6144, gsf, "gsf")):
                    def ev_g(n4, pb, pn, dstf=dstf, dn=dn):
                        ACT(dstf[:, n4 * 512:(n4 + 1) * 512], pb[:, :], AF.Tanh, [pn], [dn], scale=0.5)
                    proj_fm(col0 + dt_ * 128, ev_g)
                wa, wan = load_w(w_o_att, dt_ * 128, 128, 8)
                for n4 in range(4):
                    tk = slice(n4 * 512, (n4 + 1) * 512)
                    pb = pbank[n4 % 2]; pn = "pb%d" % (n4 % 2)
                    for hh in range(8):
                        MM(pb[:, :], wa[:, hh, 0:128], attzT[:, hh, tk], hh == 0, hh == 7, [wan, "attzT"], [pn])
                    STT(m1f[:, tk], gaf[:, tk], 1.0, pb[:, :], ADD, MUL, ["gaf", pn], ["m1f"])
                ws, wsn = load_w(w_o_ssm, dt_ * 128, 128, 4)
                for n4 in range(4):
                    tk = slice(n4 * 512, (n4 + 1) * 512)
                    pb = pbank[2 + n4 % 2]; pn = "pb%d" % (2 + n4 % 2)
                    for ft in range(4):
                        MM(pb[:, :], ws[:, ft, 0:128], ssmzT[:, ft, tk], ft == 0, ft == 3, [wsn, "ssmzT"], [pn])
                    STT(m2, gsf[:, tk], 1.0, pb[:, :], ADD, MUL, ["gsf", pn], ["m2"])
                    TT(mergedT[:, dt_, tk], m1f[:, tk], m2, ADD, ["m1f", "m2"], ["mergedT"])
            if dbg and b == 0:
                S.barrier()
                S.finals.append(DMA("sp", DBG["mergedT"], mergedT.rearrange("p a b -> p (a b)"), [], [], "dbg"))
                S.barrier()
            for cb in range(4):
                DMA("pool", wo_sb[:, :, cb * 256:(cb + 1) * 256], w_out[:, cb * 256:(cb + 1) * 256].rearrange("(kt p) c -> p kt c", p=128), [], ["wo_sb"], "wo")
            DMA("sp", xr2[0], x[b, 0:128, :], [], ["xr0"], "xr0")
            for tt in range(16):
                sl_ = tt % 2
                xr = xr2[sl_]; xo = xo2[sl_]; ob = ob2[sl_]
                xrn = "xr%d" % sl_; xon = "xo%d" % sl_; obn = "ob%d" % sl_; sqn = "ssqf%d" % sl_
                sq_ = smallf[:, 60 + 2 * sl_:62 + 2 * sl_]
                if tt + 1 < 16:
                    DMA("sp", xr2[1 - sl_], x[b, (tt + 1) * 128:(tt + 2) * 128, :], [], ["xr%d" % (1 - sl_)], "xr%d" % (1 - sl_))
                for hf in range(2):
                    pb = pbank[4 + hf + 2 * sl_]; pn = "pb%d" % (4 + hf + 2 * sl_)
                    for dt_ in range(8):
                        MM(pb[:, :], mergedT[:, dt_, tt * 128:(tt + 1) * 128], wo_sb[:, dt_, hf * 512:(hf + 1) * 512], dt_ == 0, dt_ == 7, ["mergedT", "wo_sb"], [pn])
                    STT(xo[:, hf * 512:(hf + 1) * 512], pb[:, :], 0.5, xr[:, hf * 512:(hf + 1) * 512], MUL, ADD, [pn, xrn], [xon])
                ACT(ob, xo, AF.Square, [xon], [obn, sqn], accum=sq_[:, 0:1])
                TS(sq_[:, 1:2], sq_[:, 0:1], 1.0 / D, EPS, MUL, ADD, [sqn], [sqn + "b"])
                ACT(sq_[:, 1:2], sq_[:, 1:2], AF.Sqrt, [sqn + "b"], [sqn + "b"])
                RCP(sq_[:, 1:2], sq_[:, 1:2], [sqn + "b"], [sqn + "b"])
                STT(ob, xo, sq_[:, 1:2], fgb, MUL, MUL, [xon, sqn + "b", "fgb", obn], [obn])
                fin = DMA("sp", out[b, tt * 128:(tt + 1) * 128, :], ob, [obn], [], "out")
                S.finals.append(fin)
            wp_cur[0] = wp_main
            S.barrier()
        S.emit()
    return nc


_CACHE = {}


def kernel(**inputs):
    if "nc" not in _CACHE:
        _CACHE["nc"] = build_program()
    nc = _CACHE["nc"]
    f = lambda a: np.ascontiguousarray(np.asarray(a, dtype=np.float32))
    x = f(inputs["x"])
    common = {
        "norm_g": f(inputs["norm_g"]).reshape(D),
        "w_in": f(inputs["w_in"]).reshape(D, 7168),
        "lambda_q1": f(inputs["lambda_q1"]).reshape(64), "lambda_k1": f(inputs["lambda_k1"]).reshape(64),
        "lambda_q2": f(inputs["lambda_q2"]).reshape(64), "lambda_k2": f(inputs["lambda_k2"]).reshape(64),
        "subln_g": f(inputs["subln_g"]).reshape(128),
        "w_o_att": f(inputs["w_o_att"]).reshape(D, D),
        "ssm_lambda_re": f(inputs["ssm_lambda_re"]).reshape(32, 64), "ssm_lambda_im": f(inputs["ssm_lambda_im"]).reshape(32, 64),
        "ssm_log_dt": f(inputs["ssm_log_dt"]).reshape(32),
        "ssm_b_re": f(inputs["ssm_b_re"]).reshape(32, 64, 16), "ssm_b_im": f(inputs["ssm_b_im"]).reshape(32, 64, 16),
        "ssm_c_re": f(inputs["ssm_c_re"]).reshape(32, 16, 64), "ssm_c_im": f(inputs["ssm_c_im"]).reshape(32, 16, 64),
        "ssm_d": f(inputs["ssm_d"]).reshape(512),
        "w_glu": f(inputs["w_glu"]).reshape(512, 512), "b_glu": f(inputs["b_glu"]).reshape(512),
        "w_o_ssm": f(inputs["w_o_ssm"]).reshape(512, D), "w_out": f(inputs["w_out"]).reshape(D, D),
        "final_g": f(inputs["final_g"]).reshape(D),
    }
    in_maps = []
    for c in range(8):
        m = dict(common)
        m["x"] = np.ascontiguousarray(x[2 * c:2 * c + 2])
        in_maps.append(m)
    res = run_bass_kernel_spmd(nc, in_maps, core_ids=list(range(8)))
    return np.concatenate([np.asarray(r["out"]) for r in res.results], axis=0).astype(np.float32)
```

```python
import contextlib
import math
import numpy as np
import concourse.bass as bass
import concourse.mybir as mybir
from concourse.bass_utils import run_bass_kernel_spmd

F32 = mybir.dt.float32
BF16 = mybir.dt.bfloat16
ALU = mybir.AluOpType
AF = mybir.ActivationFunctionType
AX = mybir.AxisListType

ENGS = ("pe", "act", "dve", "pool", "sp")
SEM_CAP = 1800
L = 2048
D = 1024
NSEQ = 2
EPS = 1e-5


class Op:
    __slots__ = ("eng", "fn", "deps", "signal", "sem", "val", "is_dma", "dkey", "idx")

    def __init__(self, eng, fn, is_dma, dkey):
        self.eng = eng
        self.fn = fn
        self.deps = []
        self.signal = is_dma
        self.sem = None
        self.val = 0
        self.is_dma = is_dma
        self.dkey = dkey


class Sched:
    def __init__(self, nc):
        self.nc = nc
        self.ops = {e: [] for e in ENGS}
        self.last_w = {}
        self.readers = {}
        self.all_ops = []
        self.bar_deps = {e: [] for e in ENGS}
        self.pending_dma = []
        self.finals = []

    def barrier(self):
        deps = []
        for e in ENGS:
            for op in reversed(self.ops[e]):
                if not op.is_dma:
                    deps.append(op)
                    break
        last_per_key = {}
        for op in self.pending_dma:
            last_per_key[op.dkey] = op
        deps.extend(last_per_key.values())
        self.pending_dma = []
        for e in ENGS:
            self.bar_deps[e] = list(deps)
        self.last_w = {}
        self.readers = {}

    def add(self, eng, fn, reads=(), writes=(), dma=None):
        op = Op(eng, fn, dma is not None, dma)
        deps = list(self.bar_deps[eng])
        self.bar_deps[eng] = []
        for r in reads:
            w = self.last_w.get(r)
            if w is not None:
                deps.append(w)
        for r in writes:
            w = self.last_w.get(r)
            if w is not None:
                deps.append(w)
            deps.extend(self.readers.get(r, ()))
        op.idx = len(self.all_ops)
        seen = set()
        latest = {}
        keep = []
        for d in deps:
            if id(d) in seen or d is op:
                continue
            seen.add(id(d))
            if d.eng == "pe" and eng == "pe" and not d.is_dma and dma is None:
                continue
            if not d.is_dma and d.eng in ("pe", "act", "dve"):
                if d.eng not in latest or d.idx > latest[d.eng].idx:
                    latest[d.eng] = d
                continue
            keep.append(d)
        keep.extend(latest.values())
        for d in keep:
            op.deps.append(d)
            d.signal = True
        for r in reads:
            self.readers.setdefault(r, []).append(op)
        for r in writes:
            self.last_w[r] = op
            self.readers[r] = []
        self.ops[eng].append(op)
        self.all_ops.append(op)
        if op.is_dma:
            self.pending_dma.append(op)
        return op

    def emit(self):
        nc = self.nc
        sem_specs = []
        cur = {}
        for op in self.all_ops:
            if not op.signal:
                continue
            key = ("dma", op.dkey) if op.is_dma else ("eng", op.eng)
            if key not in cur or cur[key][1] >= SEM_CAP:
                sem_specs.append("s%d_%s" % (len(sem_specs), str(key[1])[:12]))
                cur[key] = [len(sem_specs) - 1, 0]
            cur[key][1] += 1
            op.sem = cur[key][0]
            op.val = cur[key][1] * (16 if op.is_dma else 1)
        with contextlib.ExitStack() as es:
            sems = [es.enter_context(nc.semaphore(n)) for n in sem_specs]
            block = es.enter_context(nc.Block())
            finals = self.finals

            def run(eng_name):
                def body(eng):
                    waited = {}

                    def w(d):
                        if waited.get(d.sem, 0) >= d.val:
                            return
                        eng.wait_ge(sems[d.sem], d.val)
                        waited[d.sem] = d.val

                    for op in self.ops[eng_name]:
                        for d in op.deps:
                            w(d)
                        ins = op.fn(eng)
                        if op.signal:
                            ins.then_inc(sems[op.sem], 16 if op.is_dma else 1)
                    if eng_name == "sp":
                        fmax = {}
                        for d in finals:
                            if d.sem not in fmax or d.val > fmax[d.sem].val:
                                fmax[d.sem] = d
                        for d in fmax.values():
                            w(d)

                return body

            block.tensor(run("pe"))
            block.scalar(run("act"))
            block.vector(run("dve"))
            block.gpsimd(run("pool"))
            block.sync(run("sp"))
        return len(sem_specs)


def build_program(dbg=False):
    nc = bass.Bass("TRN2", target_bir_lowering=False)
    DBG = {}

    def din(name, shape):
        return nc.dram_tensor(name, shape, F32, kind="ExternalInput").ap()

    x = din("x", [NSEQ, L, D])
    norm_g = din("norm_g", [D])
    w_in = din("w_in", [D, 7168])
    lq1 = din("lambda_q1", [64]); lk1 = din("lambda_k1", [64])
    lq2 = din("lambda_q2", [64]); lk2 = din("lambda_k2", [64])
    subln_g = din("subln_g", [128])
    w_o_att = din("w_o_att", [D, D])
    s_lre = din("ssm_lambda_re", [32, 64]); s_lim = din("ssm_lambda_im", [32, 64])
    s_ldt = din("ssm_log_dt", [32])
    s_bre = din("ssm_b_re", [32, 64, 16]); s_bim = din("ssm_b_im", [32, 64, 16])
    s_cre = din("ssm_c_re", [32, 16, 64]); s_cim = din("ssm_c_im", [32, 16, 64])
    s_d = din("ssm_d", [512])
    w_glu = din("w_glu", [512, 512]); b_glu = din("b_glu", [512])
    w_o_ssm = din("w_o_ssm", [512, D]); w_out = din("w_out", [D, D])
    final_g = din("final_g", [D])
    out = nc.dram_tensor("out", [NSEQ, L, D], F32, kind="ExternalOutput").ap()
    CW = 18432
    cscr = nc.dram_tensor("cscr", [128, CW], F32, kind="Internal").ap()
    if dbg:
        DBG["consts"] = nc.dram_tensor("d_consts", [128, CW], F32, kind="ExternalOutput").ap()
        DBG["hT"] = nc.dram_tensor("d_hT", [128, 8 * L], BF16, kind="ExternalOutput").ap()
        DBG["ssmzT"] = nc.dram_tensor("d_ssmzT", [128, 4 * L], BF16, kind="ExternalOutput").ap()
        DBG["gT"] = nc.dram_tensor("d_gT", [128, 4 * L], BF16, kind="ExternalOutput").ap()
        DBG["attzT"] = nc.dram_tensor("d_attzT", [128, 8 * L], BF16, kind="ExternalOutput").ap()
        DBG["mergedT"] = nc.dram_tensor("d_mergedT", [128, 8 * L], BF16, kind="ExternalOutput").ap()
        DBG["small"] = nc.dram_tensor("d_small", [128, 64], F32, kind="ExternalOutput").ap()
        DBG["bias_tab"] = nc.dram_tensor("d_bias_tab", [128, 128], F32, kind="ExternalOutput").ap()
        DBG["misc"] = nc.dram_tensor("d_misc", [128, 16], F32, kind="ExternalOutput").ap()

    es = contextlib.ExitStack()
    with es:
        def T(name, shape, dt):
            return es.enter_context(nc.sbuf_tensor(name, shape, dt))

        def PS(name, shape, dt):
            return es.enter_context(nc.psum_tensor(name, shape, dt))

        hT = T("hT", [128, 8, L], BF16)
        attzT = T("attzT", [128, 8, L], BF16)
        ssmzT = T("ssmzT", [128, 4, L], BF16)
        wblk = [T("wblk%d" % i, [128, 8, 256], BF16) for i in range(2)]
        ident_f = T("ident_f", [128, 128], F32)
        ident_b = T("ident_b", [128, 128], BF16)
        mask_b = T("mask_b", [128, 128], BF16)
        orow_b = T("orow_b", [128, 128], BF16)
        cb_b = T("cb_b", [128, 2, 512], BF16)
        bias_tab = T("bias_tab", [128, 8, 16], F32)
        g_col = T("g_col", [128, 8], F32)
        sgc = T("sgc", [128, 1], F32)
        nlam = T("nlam", [128, 1], F32)
        bglu_h = T("bglu_h", [128, 4], F32)
        smallf = T("smallf", [128, 64], F32)
        UW = 29440
        U = T("U", [128, UW], F32)

        def carve(off_words, shape, dt):
            n = 1
            for s_ in shape[1:]:
                n *= s_
            words = n if dt == F32 else (n + 1) // 2
            assert off_words + words <= UW, (off_words, words)
            ap = U[:, off_words:off_words + words]
            if dt != F32:
                ap = ap.bitcast(dt)
            if len(shape) == 3:
                ap = ap.rearrange("p (a b) -> p a b", a=shape[1])
            elif len(shape) == 4:
                ap = ap.rearrange("p (a b c) -> p a b c", a=shape[1], b=shape[2])
            elif len(shape) == 5:
                ap = ap.rearrange("p (a b c d) -> p a b c d", a=shape[1], b=shape[2], c=shape[3])
            return ap

        BS = carve(0, [128, 4, 8, 2, 128], BF16)
        CX = carve(4096, [128, 16, 8, 2, 32], BF16)
        Mw = carve(8192, [128, 4, 8, 128], BF16)
        Ere = carve(10240, [128, 16, 256], F32)
        Eim = carve(14336, [128, 16, 256], F32)
        CONSTS = U[:, 0:CW]
        B0 = CW

        pbank = [PS("pb%d" % i, [128, 512], F32) for i in range(8)]

        S = Sched(nc)
        A = S.add

        def MM(o, lhsT, rhs, st, sp, R, W, **kw):
            return A("pe", lambda e: e.matmul(o, lhsT=lhsT, rhs=rhs, start=st, stop=sp, **kw), reads=R, writes=W)

        def TR(o, i, idt, R, W):
            return A("pe", lambda e: e.transpose(out=o, in_=i, identity=idt), reads=R, writes=W)

        def ACT(o, i, func, R, W, bias=0.0, scale=1.0, accum=None):
            if accum is None:
                return A("act", lambda e: e.activation(out=o, in_=i, func=func, bias=bias, scale=scale), reads=R, writes=W)
            return A("act", lambda e: e.activation(out=o, in_=i, func=func, bias=bias, scale=scale, accum_out=accum), reads=R, writes=W)

        def TT(o, a, b, op, R, W, eng="dve"):
            return A(eng, lambda e: e.tensor_tensor(out=o, in0=a, in1=b, op=op), reads=R, writes=W)

        def TS(o, a, s1, s2, op0, op1, R, W, eng="dve"):
            if s2 is None and isinstance(s1, float):
                return A(eng, lambda e: e.tensor_scalar(out=o, in0=a, scalar1=s1, scalar2=0.0, op0=op0, op1=ALU.add), reads=R, writes=W)
            if s2 is None:
                return A(eng, lambda e: e.tensor_scalar(out=o, in0=a, scalar1=s1, scalar2=None, op0=op0), reads=R, writes=W)
            return A(eng, lambda e: e.tensor_scalar(out=o, in0=a, scalar1=s1, scalar2=s2, op0=op0, op1=op1), reads=R, writes=W)

        def STT(o, a, sc, b, op0, op1, R, W, eng="dve"):
            return A(eng, lambda e: e.scalar_tensor_tensor(out=o, in0=a, scalar=sc, in1=b, op0=op0, op1=op1), reads=R, writes=W)

        def CP(o, i, R, W, eng="dve"):
            return A(eng, lambda e: e.tensor_copy(out=o, in_=i), reads=R, writes=W)

        def ACP(o, i, R, W):
            return A("act", lambda e: e.activation(out=o, in_=i, func=AF.Copy), reads=R, writes=W)

        def MS(o, val, W, eng="pool"):
            return A(eng, lambda e: e.memset(o, val), writes=W)

        def RCP(o, i, R, W):
            return A("dve", lambda e: e.reciprocal(out=o, in_=i), reads=R, writes=W)

        su_ops = []

        def DMA(eng, o, i, R, W, key, slow=False):
            if key == "setup":
                key = "su%d" % len(su_ops)
                fn_ = (lambda e: e.dma_start(out=o, in_=i, allow_slow_non_contiguous=True)) if slow else (lambda e: e.dma_start(out=o, in_=i))
                op_ = A(eng, fn_, reads=R, writes=W, dma=key)
                if len(su_ops) >= 2:
                    op_.deps.append(su_ops[-2])
                su_ops.append(op_)
                return op_
            if slow:
                return A(eng, lambda e: e.dma_start(out=o, in_=i, allow_slow_non_contiguous=True), reads=R, writes=W, dma=key)
            return A(eng, lambda e: e.dma_start(out=o, in_=i), reads=R, writes=W, dma=key)

        def bc(ap, shape):
            return ap.unsqueeze(len(ap.shape)).to_broadcast(shape)

        MUL, ADD, SUB = ALU.mult, ALU.add, ALU.subtract

        so = [B0]

        def stmp(shape, dt=F32):
            ap = carve(so[0], shape, dt)
            n = 1
            for s_ in shape[1:]:
                n *= s_
            so[0] += n if dt == F32 else (n + 1) // 2
            return ap

        azf = attzT[:, :, :].rearrange("p a b -> p (a b)").bitcast(F32)
        szf = ssmzT[:, :, :].rearrange("p a b -> p (a b)").bitcast(F32)

        def big3(base, idx):
            return base[:, idx * 2048:(idx + 1) * 2048].rearrange("p (a b) -> p a b", a=16)

        ones_f = stmp([128, 128]); zer_f = stmp([128, 128]); mask_f = stmp([128, 128])
        MS(ones_f, 1.0, ["ones_f"]); MS(zer_f, 0.0, ["zer_f"])
        A("pool", lambda e: e.affine_select(out=ident_f[:], in_=ones_f, pattern=[[1, 128]], compare_op=ALU.is_equal, fill=0.0, base=0, channel_multiplier=-1), reads=["ones_f"], writes=["ident_f"])
        A("pool", lambda e: e.affine_select(out=mask_f, in_=zer_f, pattern=[[1, 128]], compare_op=ALU.is_ge, fill=-30000.0, base=0, channel_multiplier=-1), reads=["zer_f"], writes=["mask_f"])
        CP(ident_b[:], ident_f[:], ["ident_f"], ["ident_b"])
        CP(mask_b[:], mask_f, ["mask_f"], ["mask_b"])
        MS(orow_b[:], 0.0, ["orow_b"]); MS(orow_b[0:1, :], 1.0, ["orow_b"])
        MS(cb_b[:], 0.0, ["cb_b"])
        for h_ in range(2):
            for ib_ in range(3):
                MS(cb_b[0:1, h_, ib_ * 128:(ib_ + 1) * 128], float(8.0 * (2.0 ** (-(h_ + 1))) * 128.0 * (3 - ib_)), ["cb_b"])
        jm = stmp([128, 16])
        A("pool", lambda e: e.iota(jm, pattern=[[-128, 16]], base=-128, channel_multiplier=1, allow_small_or_imprecise_dtypes=True), writes=["jm"])
        for h in range(8):
            TS(bias_tab[:, h, :], jm, float(2.0 ** (-(h + 1))), None, MUL, None, ["jm"], ["bias_tab"])
        DMA("sp", g_col[:], norm_g.rearrange("(kt p) -> p kt", p=128), [], ["g_col"], "setup", slow=True)
        sg_raw = stmp([128, 1])
        DMA("sp", sg_raw, subln_g.rearrange("(p o) -> p o", o=1), [], ["sg_raw"], "setup", slow=True)
        TS(sgc[:], sg_raw, 0.4, None, MUL, None, ["sg_raw"], ["sgc"])
        bglu_raw = stmp([128, 4])
        DMA("sp", bglu_raw, b_glu.rearrange("(f p) -> p f", p=128), [], ["bglu_raw"], "setup", slow=True)
        TS(bglu_h[:], bglu_raw, 0.5, None, MUL, None, ["bglu_raw"], ["bglu_h"])
        dcol = stmp([128, 4])
        DMA("sp", dcol, s_d.rearrange("(f p) -> p f", p=128), [], ["dcol"], "setup", slow=True)
        lv = [stmp([128, 64]) for _ in range(4)]
        for t_, src in zip(lv, (lq1, lk1, lq2, lk2)):
            DMA("sp", t_, src.partition_broadcast(128), [], ["lv"], "setup")
        l12 = stmp([128, 2]); ltmp = stmp([128, 64])
        for i_ in range(2):
            TT(ltmp, lv[2 * i_], lv[2 * i_ + 1], MUL, ["lv"], ["ltmp"])
            A("dve", (lambda i_: lambda e: e.tensor_reduce(out=l12[:, i_:i_ + 1], in_=ltmp, axis=AX.X, op=ADD))(i_), reads=["ltmp"], writes=["l12"])
        ACT(l12, l12, AF.Exp, ["l12"], ["l12"])
        TT(nlam[:], l12[:, 1:2], l12[:, 0:1], SUB, ["l12"], ["nlam"])
        TS(nlam[:], nlam[:], -0.2, None, ADD, None, ["nlam"], ["nlam"])

        def sp16(name):
            return stmp([128, 16])

        lre = sp16("lre"); lim = sp16("lim"); dtt = sp16("dt")
        DMA("sp", lre, s_lre.rearrange("(pr h) p -> (h p) pr", h=2), [], ["lre"], "setup", slow=True)
        DMA("sp", lim, s_lim.rearrange("(pr h) p -> (h p) pr", h=2), [], ["lim"], "setup", slow=True)
        ldt2 = s_ldt.rearrange("(pr h) -> h pr", h=2)
        for hh in range(2):
            DMA("sp", dtt[hh * 64:(hh + 1) * 64, :], ldt2[hh].partition_broadcast(64), [], ["dtt"], "setup", slow=True)
        bre = stmp([128, 16, 16]); bim = stmp([128, 16, 16]); cre = stmp([128, 16, 16]); cim = stmp([128, 16, 16])
        for hh in range(2):
            ps_ = slice(hh * 64, (hh + 1) * 64)
            DMA("sp", bre[ps_], s_bre.rearrange("(pr h) p f -> h p pr f", h=2)[hh], [], ["bre"], "setup", slow=True)
            DMA("sp", bim[ps_], s_bim.rearrange("(pr h) p f -> h p pr f", h=2)[hh], [], ["bim"], "setup", slow=True)
        cnat_r = stmp([128, 4, 128]); cnat_i = stmp([128, 4, 128])
        for arr_, cn_, cnn in ((s_cre, cnat_r, "cnat_r"), (s_cim, cnat_i, "cnat_i")):
            src_ = arr_.rearrange("(ft gl) f p -> (gl f) ft p", ft=4)
            DMA("sp", cn_[:, :, 0:64], src_, [], [cnn], "setup")
            DMA("sp", cn_[:, :, 64:128], src_, [], [cnn], "setup")
        for ft in range(4):
            for cn_, cnn, dst_, dn_ in ((cnat_r, "cnat_r", cre, "cre"), (cnat_i, "cnat_i", cim, "cim")):
                TR(pbank[3][:, 0:128], cn_[:, ft, :], ident_f[:], [cnn, "ident_f"], ["pb3"])
                pv_ = pbank[3][:, 0:128].rearrange("p (g f) -> p g f", g=8)
                CP(dst_[0:64, 4 * ft:4 * ft + 4, :], pv_[0:64, 0:8:2, :], ["pb3"], [dn_])
                CP(dst_[64:128, 4 * ft:4 * ft + 4, :], pv_[64:128, 1:8:2, :], ["pb3"], [dn_])
        ACT(dtt, dtt, AF.Exp, ["dtt"], ["dtt"])
        TS(lre, lre, -1e-4, None, ALU.min, None, ["lre"], ["lre"])
        av = sp16("a"); th = sp16("th"); mag = sp16("mag")
        TT(av, lre, dtt, MUL, ["lre", "dtt"], ["av"])
        TT(th, lim, dtt, MUL, ["lim", "dtt"], ["th"])
        ACT(mag, av, AF.Exp, ["av"], ["mag"])
        halfpi = sp16("halfpi")
        MS(halfpi, float(math.pi / 2), ["halfpi"])
        cs = sp16("cs"); sn = sp16("sn"); t16a = sp16("t16a"); t16b = sp16("t16b"); t16c = sp16("t16c")
        ACT(sn, th, AF.Sin, ["th"], ["sn"], scale=1.0 / 32)
        TS(t16a, th, 1.0 / 32, None, MUL, None, ["th"], ["t16a"])
        TT(t16a, t16a, halfpi, ADD, ["t16a", "halfpi"], ["t16a"])
        ACT(cs, t16a, AF.Sin, ["t16a"], ["cs"])

        def csquare(re_, im_, rn, in_):
            TT(t16a, re_, re_, MUL, [rn], ["t16a"])
            TT(t16b, im_, im_, MUL, [in_], ["t16b"])
            TT(t16c, re_, im_, MUL, [rn, in_], ["t16c"])
            TT(re_, t16a, t16b, SUB, ["t16a", "t16b"], [rn])
            TS(im_, t16c, 2.0, None, MUL, None, ["t16c"], [in_])

        for _ in range(5):
            csquare(cs, sn, "cs", "sn")
        lam_re = sp16("lam_re"); lam_im = sp16("lam_im")
        TT(lam_re, mag, cs, MUL, ["mag", "cs"], ["lam_re"])
        TT(lam_im, mag, sn, MUL, ["mag", "sn"], ["lam_im"])
        num = sp16("num"); den = sp16("den"); cfr = sp16("cfr"); cfi = sp16("cfi")
        TS(num, lam_re, -1.0, None, ADD, None, ["lam_re"], ["num"])
        TT(t16a, lre, lre, MUL, ["lre"], ["t16a"])
        TT(t16b, lim, lim, MUL, ["lim"], ["t16b"])
        TT(den, t16a, t16b, ADD, ["t16a", "t16b"], ["den"])
        RCP(den, den, ["den"], ["den"])
        TT(t16a, num, lre, MUL, ["num", "lre"], ["t16a"])
        TT(t16b, lam_im, lim, MUL, ["lam_im", "lim"], ["t16b"])
        TT(cfr, t16a, t16b, ADD, ["t16a", "t16b"], ["cfr"])
        TT(cfr, cfr, den, MUL, ["cfr", "den"], ["cfr"])
        TT(t16a, lam_im, lre, MUL, ["lam_im", "lre"], ["t16a"])
        TT(t16b, num, lim, MUL, ["num", "lim"], ["t16b"])
        TT(cfi, t16a, t16b, SUB, ["t16a", "t16b"], ["cfi"])
        TT(cfi, cfi, den, MUL, ["cfi", "den"], ["cfi"])
        bbr = stmp([128, 16, 16]); bbi = stmp([128, 16, 16]); w1 = stmp([128, 16, 16]); w2 = stmp([128, 16, 16]); w3 = stmp([128, 16, 16]); w4 = stmp([128, 16, 16])
        S3 = [128, 16, 16]

        def cmul3(ore, oim, ar, ai, arn, ain, br, bi, brn, bin_, orn, oin, neg_im=False):
            TT(w1, br, bc(ar, S3), MUL, [brn, arn], ["w1"])
            TT(w2, bi, bc(ai, S3), MUL, [bin_, ain], ["w2"])
            TT(ore, w1, w2, SUB, ["w1", "w2"], [orn])
            TT(w3, bi, bc(ar, S3), MUL, [bin_, arn], ["w3"], eng="pool")
            TT(w4, br, bc(ai, S3), MUL, [brn, ain], ["w4"], eng="pool")
            TT(oim, w3, w4, ADD, ["w3", "w4"], [oin], eng="pool")
            if neg_im:
                TS(oim, oim, -1.0, None, MUL, None, [oin], [oin], eng="pool")

        cmul3(bbr, bbi, cfr, cfi, "cfr", "cfi", bre, bim, "bre", "bim", "bbr", "bbi")
        pwr = stmp([128, 16, 9]); pwi = stmp([128, 16, 9])
        MS(pwr[:, :, 0], 1.0, ["pwr"]); MS(pwi[:, :, 0], 0.0, ["pwi"])
        for k in range(1, 9):
            TT(t16a, pwr[:, :, k - 1], lam_re, MUL, ["pwr", "lam_re"], ["t16a"])
            TT(t16b, pwi[:, :, k - 1], lam_im, MUL, ["pwi", "lam_im"], ["t16b"])
            TT(pwr[:, :, k], t16a, t16b, SUB, ["t16a", "t16b"], ["pwr"])
            TT(t16a, pwr[:, :, k - 1], lam_im, MUL, ["pwr", "lam_im"], ["t16a"])
            TT(t16b, pwi[:, :, k - 1], lam_re, MUL, ["pwi", "lam_re"], ["t16b"])
            TT(pwi[:, :, k], t16a, t16b, ADD, ["t16a", "t16b"], ["pwi"])
        GPr, GPi, CPr, CPn = [hT[:, i_, :].rearrange("p (a b) -> p a b", a=16) for i_ in range(4)]
        for t_, nm in ((GPr, "GPr"), (GPi, "GPi"), (CPr, "CPr"), (CPn, "CPn")):
            MS(t_, 0.0, [nm])
        MS(CX, 0.0, ["CX"])
        Gr = stmp([128, 16, 16]); Gi = stmp([128, 16, 16])
        Gr_b = stmp([128, 16, 16]); Gi_b = stmp([128, 16, 16])
        GPr_b, GPi_b = [hT[:, i_, :].rearrange("p (a b) -> p a b", a=16) for i_ in (4, 5)]
        MS(GPr_b, 0.0, ["GPr_b"]); MS(GPi_b, 0.0, ["GPi_b"])

        def place(dst, dn, src, sn_):
            dps = dst.ap[0][0]; sps = src.ap[0][0]
            for hh in range(2):
                d_ap = bass.AP(dst.tensor, dst.offset + hh * 64 * dps + 16 * hh, [[dps, 64], [512, 4], [160, 4], [1, 16]])
                s_ap = bass.AP(src.tensor, src.offset + hh * 64 * sps, [[sps, 64], [64, 4], [16, 4], [1, 16]])
                ACP(d_ap, s_ap, [sn_], [dn])

        TS(w1, cim, -1.0, None, MUL, None, ["cim"], ["w1"])
        place(CPr, "CPr", cre, "cre")
        place(CPn, "CPn", w1, "w1")
        diagD = stmp([128, 4, 128])
        for ft in range(4):
            TS(diagD[:, ft, :], ident_f[:], dcol[:, ft:ft + 1], None, MUL, None, ["ident_f", "dcol"], ["diagD"])
        for k in range(8):
            if k % 2 == 0:
                Gr_, Gi_, grn, gin_, GPr_, GPi_, gprn, gpin = Gr, Gi, "Gr", "Gi", GPr, GPi, "GPr", "GPi"
            else:
                Gr_, Gi_, grn, gin_, GPr_, GPi_, gprn, gpin = Gr_b, Gi_b, "Gr_b", "Gi_b", GPr_b, GPi_b, "GPr_b", "GPi_b"
            cmul3(Gr_, Gi_, pwr[:, :, k], pwi[:, :, k], "pwr", "pwi", bbr, bbi, "bbr", "bbi", grn, gin_)
            place(GPr_, gprn, Gr_, grn)
            place(GPi_, gpin, Gi_, gin_)
            for ri, (GP, gn) in enumerate(((GPr_, gprn), (GPi_, gpin))):
                pb = pbank[ri + 4 * (k % 2)]
                pbn = "pb%d" % (ri + 4 * (k % 2))
                for ft in range(4):
                    for q in range(4):
                        MM(pb[:, ft * 128:(ft + 1) * 128], GP[:, ft * 4 + q, :], ident_b[:], q == 0, q == 3, [gn, "ident_b"], [pbn], skip_group_check=True)
                ACP(BS[:, :, 7 - k, ri, :], pb[:, :].rearrange("p (f c) -> p f c", f=4), [pbn], ["BS"])
            pb = pbank[2 + 4 * (k % 2)]
            pbn = "pb%d" % (2 + 4 * (k % 2))
            for ft in range(4):
                for q in range(4):
                    MM(pb[:, ft * 128:(ft + 1) * 128], GPr_[:, ft * 4 + q, :], CPr[:, ft * 4 + q, :], q == 0, False, [gprn, "CPr"], [pbn], skip_group_check=True)
                    MM(pb[:, ft * 128:(ft + 1) * 128], GPi_[:, ft * 4 + q, :], CPn[:, ft * 4 + q, :], False, q == 3, [gpin, "CPn"], [pbn], skip_group_check=True)
            pb3 = pb[:, :].rearrange("p (f c) -> p f c", f=4)
            if k == 0:
                TT(Mw[:, :, k, :], pb3, diagD, ADD, [pbn, "diagD"], ["Mw"])
            else:
                ACP(Mw[:, :, k, :], pb3, [pbn], ["Mw"])
        for tp in range(8):
            cmul3(Gr, Gi, pwr[:, :, tp + 1], pwi[:, :, tp + 1], "pwr", "pwi", cre, cim, "cre", "cim", "Gr", "Gi", neg_im=True)
            for ri, (G_, gn) in enumerate(((Gr, "Gr"), (Gi, "Gi"))):
                for hh in range(2):
                    ACP(CX[hh * 64:(hh + 1) * 64, :, tp, ri, 16 * hh:16 * hh + 16], G_[hh * 64:(hh + 1) * 64, :, :], [gn], ["CX"])
        Rm = sp16("Rm"); nur = sp16("nur"); nui = sp16("nui")
        TT(Rm, mag, mag, MUL, ["mag"], ["Rm"])
        TT(Rm, Rm, Rm, MUL, ["Rm"], ["Rm"])
        TT(Rm, Rm, Rm, MUL, ["Rm"], ["Rm"])
        RCP(t16a, Rm, ["Rm"], ["t16a"])
        TT(nur, pwr[:, :, 8], t16a, MUL, ["pwr", "t16a"], ["nur"])
        TT(nui, pwi[:, :, 8], t16a, MUL, ["pwi", "t16a"], ["nui"])
        TS(nui, nui, -1.0, None, MUL, None, ["nui"], ["nui"])
        MS(Ere[:, :, 0:1], 1.0, ["Ere"]); MS(Eim[:, :, 0:1], 0.0, ["Eim"])
        e1 = big3(szf, 0); e2 = big3(szf, 1); e3 = big3(azf, 0); e4 = big3(azf, 1)
        for k in range(8):
            n = 1 << k
            sh = [128, 16, n]
            TT(e1[:, :, 0:n], Ere[:, :, 0:n], bc(nur, sh), MUL, ["Ere", "nur"], ["e1"])
            TT(e2[:, :, 0:n], Eim[:, :, 0:n], bc(nui, sh), MUL, ["Eim", "nui"], ["e2"])
            TT(Ere[:, :, n:2 * n], e1[:, :, 0:n], e2[:, :, 0:n], SUB, ["e1", "e2"], ["Ere"])
            TT(e3[:, :, 0:n], Ere[:, :, 0:n], bc(nui, sh), MUL, ["Ere", "nui"], ["e3"], eng="pool")
            TT(e4[:, :, 0:n], Eim[:, :, 0:n], bc(nur, sh), MUL, ["Eim", "nur"], ["e4"], eng="pool")
            TT(Eim[:, :, n:2 * n], e3[:, :, 0:n], e4[:, :, 0:n], ADD, ["e3", "e4"], ["Eim"], eng="pool")
            if k < 7:
                csquare(nur, nui, "nur", "nui")
        Rkeep = smallf[:, 0:16]
        CP(Rkeep, Rm, ["Rm"], ["Rkeep"])
        S.barrier()
        DMA("sp", cscr, CONSTS, [], [], "cst")
        if dbg:
            S.finals.append(DMA("sp", DBG["consts"], CONSTS, [], [], "dbg"))
            S.finals.append(DMA("sp", DBG["bias_tab"], bias_tab[:, :, :].rearrange("p a b -> p (a b)"), [], [], "dbg"))
            S.finals.append(DMA("sp", DBG["misc"][:, 0:8], g_col[:], [], [], "dbg"))
            S.finals.append(DMA("sp", DBG["misc"][:, 8:9], nlam[:], [], [], "dbg", slow=True))
            S.finals.append(DMA("sp", DBG["misc"][:, 9:10], sgc[:], [], [], "dbg", slow=True))
            S.finals.append(DMA("sp", DBG["misc"][:, 10:14], bglu_h[:], [], [], "dbg"))
        S.barrier()

        class WPool:
            def __init__(self, bufs, tag):
                self.bufs = bufs; self.tag = tag; self.n = 0

            def load(self, src2d, c0, ncols, nkt):
                i = self.n % len(self.bufs)
                self.n += 1
                wb = self.bufs[i]
                nm = "%s%d" % (self.tag, i)
                srcap = src2d[:, c0:c0 + ncols].rearrange("(kt p) c -> p kt c", p=128)
                DMA("pool", wb[:, 0:nkt, 0:ncols], srcap, [], [nm], "w" + nm)
                return wb, nm

        wp_main = WPool([w_[:, :, :] for w_ in wblk], "wblk")
        wp_cur = [wp_main]

        def load_w(src2d, c0, ncols, nkt):
            return wp_cur[0].load(src2d, c0, ncols, nkt)

        def proj_fm(col0, evac):
            wb, wn = load_w(w_in, col0, 128, 8)
            for n4 in range(4):
                pb = pbank[6 + (n4 % 2)]
                pn = "pb%d" % (6 + (n4 % 2))
                for kt in range(8):
                    MM(pb[:, :], wb[:, kt, 0:128], hT[:, kt, n4 * 512:(n4 + 1) * 512], kt == 0, kt == 7, [wn, "hT"], [pn])
                evac(n4, pb, pn)

        for b in range(NSEQ):
            o_ = B0
            xt = [carve(o_ + i * 1024, [128, 1024], F32) for i in range(2)]
            xsq2 = [carve(o_ + 2048 + i * 1024, [128, 1024], F32) for i in range(2)]
            xn2 = [carve(o_ + 4096 + i * 512, [128, 1024], BF16) for i in range(2)]
            ssq = smallf[:, 16:18]
            DMA("sp", xt[0], x[b, 0:128, :], [], ["xt0"], "x0")
            for tt in range(16):
                sl_ = tt % 2
                xb = xt[sl_]; xbn = "xt%d" % sl_
                xsq = xsq2[sl_]; xn = xn2[sl_]; xnn = "xn%d" % sl_
                sq_ = smallf[:, 60 + 2 * sl_:62 + 2 * sl_]; sqn = "ssqa%d" % sl_
                if tt + 1 < 16:
                    DMA("sp", xt[1 - sl_], x[b, (tt + 1) * 128:(tt + 2) * 128, :], [], ["xt%d" % (1 - sl_)], "x%d" % (1 - sl_))
                ACT(xsq, xb, AF.Square, [xbn], ["xsq%d" % sl_, sqn], accum=sq_[:, 0:1])
                TS(sq_[:, 1:2], sq_[:, 0:1], 1.0 / D, EPS, MUL, ADD, [sqn], [sqn + "b"])
                ACT(sq_[:, 1:2], sq_[:, 1:2], AF.Sqrt, [sqn + "b"], [sqn + "b"])
                RCP(sq_[:, 1:2], sq_[:, 1:2], [sqn + "b"], [sqn + "b"])
                TS(xn, xb, sq_[:, 1:2], None, MUL, None, [xbn, sqn + "b"], [xnn])
                ptr = pbank[6 + sl_][:, :].bitcast(BF16).rearrange("p (a b) -> p a b", a=8)
                pn_ = "pb%d" % (6 + sl_)
                for kt in range(8):
                    TR(ptr[:, kt, :], xn[:, kt * 128:(kt + 1) * 128], ident_b[:], [xnn, "ident_b"], [pn_])
                TT(hT[:, :, tt * 128:(tt + 1) * 128], ptr, bc(g_col[:], [128, 8, 128]), MUL, [pn_, "g_col"], ["hT"])
            S.barrier()
            if dbg and b == 0:
                S.finals.append(DMA("sp", DBG["hT"], hT[:, :, :].rearrange("p a b -> p (a b)"), [], [], "dbg"))
                S.barrier()

            if b > 0:
                DMA("sp", CONSTS, cscr, [], [], "cst")
                S.barrier()
            o_ = CW
            uT = carve(o_, [128, L], BF16); o_ += 1024
            zs2 = carve(o_, [128, L], BF16); o_ += 1024
            rt = []
            for i in range(6):
                rt.append(carve(o_, [128, 4, 256], F32)); o_ += 1024
            Xs = carve(o_, [128, 4, 2, 258], BF16); o_ += 1032
            Rdec = carve(o_, [128, 4, 256], F32); o_ += 1024
            tmpf = carve(o_, [128, 512], F32); o_ += 512
            az = attzT[:, :, :].rearrange("p a b -> p (a b)")
            gT = az[:, 0:8192].rearrange("p (a b) -> p a b", a=4)
            ysf = az[:, 8192:12288].bitcast(F32)
            gl1 = az[:, 12288:14336].bitcast(F32)
            gl2 = az[:, 14336:16384].bitcast(F32)
            MS(Xs, 0.0, ["Xs"])
            u_sl = [(uT, "uT"), (zs2, "zs2")]
            r2 = lambda a_: a_.rearrange("p a b -> p (a b)")

            def s5_a(ft):
                ub, un = u_sl[ft % 2]

                ubd = ub.rearrange("p (s c) -> p s c", s=8)

                def ev_u(n4, pb, pn):
                    CP(ubd[:, :, n4 * 64:(n4 + 1) * 64], pb[:, :].rearrange("p (c s) -> p s c", s=8), [pn], [un])
                proj_fm(4096 + ft * 128, ev_u)

            def s5_b(ft):
                ub, un = u_sl[ft % 2]
                ubd = ub.rearrange("p (s c) -> p s c", s=8)
                for q in range(4):
                    pb = pbank[2 + q]; pn = "pb%d" % (2 + q)
                    for ri in range(2):
                        for sp_ in range(8):
                            MM(pb[:, ri * 256:(ri + 1) * 256], BS[32 * q:32 * q + 32, ft, sp_, ri, :],
                               ubd[32 * q:32 * q + 32, sp_, :], sp_ == 0, sp_ == 7, ["BS", un], [pn],
                               skip_group_check=True, tile_position=(32 * q, 0))

            def s5_c(ft):
                Er = Ere[:, ft * 4:(ft + 1) * 4, :]; Ei = Eim[:, ft * 4:(ft + 1) * 4, :]
                for q in range(4):
                    pb = pbank[2 + q]; pn = "pb%d" % (2 + q)
                    sre = pb[:, 0:256]; sim = pb[:, 256:512]
                    TT(rt[0][:, q, :], sre, Er[:, q, :], MUL, [pn], ["rt0"])
                    TT(rt[1][:, q, :], sim, Ei[:, q, :], MUL, [pn], ["rt1"])
                    TT(rt[2][:, q, :], sim, Er[:, q, :], MUL, [pn], ["rt2"])
                    TT(rt[3][:, q, :], sre, Ei[:, q, :], MUL, [pn], ["rt3"])

            def s5_de(ft):
                Er = Ere[:, ft * 4:(ft + 1) * 4, :]; Ei = Eim[:, ft * 4:(ft + 1) * 4, :]
                TT(rt[0], rt[0], rt[1], SUB, ["rt0", "rt1"], ["rt0"])
                TT(rt[2], rt[2], rt[3], ADD, ["rt2", "rt3"], ["rt2"])
                CP(Rdec, bc(Rkeep[:, ft * 4:(ft + 1) * 4], [128, 4, 256]), ["Rkeep"], ["Rdec"])
                MS(Rdec[:, :, 0:1], 0.0, ["Rdec"], eng="dve")
                A("dve", lambda e: e.tensor_tensor_scan(out=r2(rt[1]), data0=r2(Rdec), data1=r2(rt[0]), initial=0.0, op0=MUL, op1=ADD), reads=["Rdec", "rt0"], writes=["rt1"])
                A("dve", lambda e: e.tensor_tensor_scan(out=r2(rt[3]), data0=r2(Rdec), data1=r2(rt[2]), initial=0.0, op0=MUL, op1=ADD), reads=["Rdec", "rt2"], writes=["rt3"])
                TT(rt[0], rt[1], Er, MUL, ["rt1"], ["rt0"])
                TT(rt[2], rt[3], Ei, MUL, ["rt3"], ["rt2"])
                TT(Xs[:, :, 0, 1:256], rt[0][:, :, 0:255], rt[2][:, :, 0:255], ADD, ["rt0", "rt2"], ["Xs"])
                TT(rt[4], rt[3], Er, MUL, ["rt3"], ["rt4"])
                TT(rt[5], rt[1], Ei, MUL, ["rt1"], ["rt5"])
                TT(Xs[:, :, 1, 1:256], rt[4][:, :, 0:255], rt[5][:, :, 0:255], SUB, ["rt4", "rt5"], ["Xs"])

            def s5_f(ft):
                ub, un = u_sl[ft % 2]
                ubd = ub.rearrange("p (s c) -> p s c", s=8)
                for tp in range(8):
                    pb = pbank[tp % 2]; pn = "pb%d" % (tp % 2)
                    for sp_ in range(tp + 1):
                        MM(pb[:, 0:256], Mw[:, ft, tp - sp_, :], ubd[:, sp_, :], sp_ == 0, False, ["Mw", un], [pn], skip_group_check=True)
                    for q in range(4):
                        for ri in range(2):
                            MM(pb[32 * q:32 * q + 32, 0:256], CX[:, ft * 4 + q, tp, ri, :], Xs[:, q, ri, 0:256], False,
                               (q == 3 and ri == 1), ["CX", "Xs"], [pn], skip_group_check=True, tile_position=(0, 32 * q))
                    CP(ysf[:, tp:L:8], pb[:, 0:256], [pn], ["ysf"])

            def s5_g(ft):
                for hf in range(2):
                    yv = ysf[:, hf * 1024:(hf + 1) * 1024]
                    TT(gl1, yv, yv, MUL, ["ysf"], ["gl1"])
                    TS(gl1, gl1, 0.044715, 1.0, MUL, ADD, ["gl1"], ["gl1"])
                    TT(gl1, gl1, yv, MUL, ["gl1", "ysf"], ["gl1"])
                    ACT(gl2, gl1, AF.Tanh, ["gl1"], ["gl2"], scale=0.7978845608028654)
                    STT(gT[:, ft, hf * 1024:(hf + 1) * 1024], gl2, 1.0, yv, ADD, MUL, ["gl2", "ysf"], ["gT"])

            s5_a(0); s5_b(0)
            for ft in range(4):
                if ft + 1 < 4:
                    s5_a(ft + 1)
                s5_c(ft)
                if ft + 1 < 4:
                    s5_b(ft + 1)
                s5_de(ft)
                s5_f(ft)
                s5_g(ft)
            gtmp = [(tmpf, "tmpf"), (gl1[:, 0:512], "gl1"), (gl2[:, 0:512], "gl2")]
            gcnt = [0]

            def nxt_tmp():
                gcnt[0] += 1
                return gtmp[gcnt[0] % 3]

            for fo in range(4):
                def ev_z(n4, pb, pn):
                    tf_, tfn = nxt_tmp()
                    ACT(tf_, pb[:, :], AF.Tanh, [pn], [tfn], scale=0.5)
                    STT(zs2[:, n4 * 512:(n4 + 1) * 512], tf_, 1.0, pb[:, :], ADD, MUL, [tfn, pn], ["zs2"])
                proj_fm(4608 + fo * 128, ev_z)
                wb, wn = load_w(w_glu, fo * 128, 128, 4)
                for n4 in range(4):
                    pb = pbank[n4 % 2]; pn = "pb%d" % (n4 % 2)
                    for ft in range(4):
                        MM(pb[:, :], wb[:, ft, 0:128], gT[:, ft, n4 * 512:(n4 + 1) * 512], ft == 0, ft == 3, [wn, "gT"], [pn])
                    tf_, tfn = nxt_tmp()
                    ACT(tf_, pb[:, :], AF.Tanh, [pn, "bglu_h"], [tfn], bias=bglu_h[:, fo:fo + 1], scale=0.25)
                    STT(tf_, tf_, 1.0, gT[:, fo, n4 * 512:(n4 + 1) * 512], ADD, MUL, [tfn, "gT"], [tfn])
                    STT(ssmzT[:, fo, n4 * 512:(n4 + 1) * 512], tf_, 0.125, zs2[:, n4 * 512:(n4 + 1) * 512], MUL, MUL, [tfn, "zs2"], ["ssmzT"])
            S.barrier()
            if dbg and b == 0:
                S.finals.append(DMA("sp", DBG["ssmzT"], ssmzT[:, :, :].rearrange("p a b -> p (a b)"), [], [], "dbg"))
                S.finals.append(DMA("sp", DBG["gT"], az[:, 0:8192], [], [], "dbg"))
                S.barrier()

            o_ = 0
            qT = [carve(o_ + i * 1024, [128, L], BF16) for i in range(2)]; o_ += 2048
            kT = [[carve(o_ + (2 * i + c_) * 1024, [128, L], BF16) for c_ in range(2)] for i in range(2)]; o_ += 4096
            for i in range(2):
                MS(kT[i][0][64:128, :], 0.0, ["kT%d" % i])
                MS(kT[i][1][0:64, :], 0.0, ["kT%d" % i])
            zsil = [carve(o_ + i * 1024, [128, L], BF16) for i in range(2)]; o_ += 2048
            v_sb = carve(o_, [128, 16, 8, 130], BF16); o_ += 8320
            Pb = [carve(o_ + i * 256, [128, 512], BF16) for i in range(3)]; o_ += 768
            osb = [carve(o_ + i * 520, [128, 4, 130], F32) for i in range(2)]; o_ += 1040
            ofa = [carve(o_ + i * 2048, [128, 4, 4, 128], F32) for i in range(2)]; o_ += 4096
            ot_ = carve(o_, [128, 4, 128], F32); o_ += 512
            otmp = carve(o_, [128, 4, 128], F32); o_ += 512
            onb = carve(o_, [128, 4, 128], BF16); o_ += 256
            tmpa = [carve(o_ + i * 512, [128, 512], F32) for i in range(2)]; o_ += 1024
            awb = [carve(o_ + i * 1024, [128, 8, 256], BF16) for i in range(4)]; o_ += 4096
            wp_cur[0] = WPool(awb, "awb")
            rr = smallf[:, 20:28]
            ssa = [smallf[:, 28:44], smallf[:, 44:60]]
            S4 = [128, 4, 128]
            MS(v_sb[:, :, :, 128:129], 1.0, ["v_sb"])
            for vb_ in range(4):
                wb, wn = load_w(w_in, 2048 + vb_ * 256, 256, 8)
                for tt in range(16):
                    pb = pbank[6 + (tt % 2)]; pn = "pb%d" % (6 + (tt % 2))
                    for kt in range(8):
                        MM(pb[:, 0:256], hT[:, kt, tt * 128:(tt + 1) * 128], wb[:, kt, 0:256], kt == 0, kt == 7, ["hT", wn], [pn])
                    CP(v_sb[:, tt, 2 * vb_:2 * vb_ + 2, 0:128], pb[:, 0:256].rearrange("p (a b) -> p a b", a=2), [pn], ["v_sb"])

            onb2 = [onb, carve(o_, [128, 4, 128], BF16)]; o_ += 256
            ucnt = [0]

            def proj_units(h, slot):
                units = []
                for kind, col0 in (("q", h * 128), ("k", 1024 + h * 128), ("z", 3072 + h * 128)):
                    wref = {}
                    for n8 in range(4):
                        hf = ucnt[0] % 2
                        ucnt[0] += 1
                        pbh = pbank[6 + hf][:, :]; pn = "pb%d" % (6 + hf)
                        sl = slice(n8 * 512, (n8 + 1) * 512)

                        def stA(kind=kind, col0=col0, n8=n8, wref=wref):
                            if n8 == 0:
                                wref["w"] = load_w(w_in, col0, 128, 8)

                        def stB(pbh=pbh, pn=pn, sl=sl, wref=wref):
                            wb, wn = wref["w"]
                            for kt in range(8):
                                MM(pbh, wb[:, kt, 0:128], hT[:, kt, sl], kt == 0, kt == 7, [wn, "hT"], [pn])

                        def stC(kind=kind, pbh=pbh, pn=pn, sl=sl, n8=n8):
                            if kind == "q":
                                CP(qT[slot][:, sl], pbh, [pn], ["qT%d" % slot])
                            elif kind == "k":
                                CP(kT[slot][0][0:64, sl], pbh[0:64, :], [pn], ["kT%d" % slot])
                                CP(kT[slot][1][64:128, sl], pbh[64:128, :], [pn], ["kT%d" % slot])
                            else:
                                ta = tmpa[n8 % 2]; tn = "tmpa%d" % (n8 % 2)
                                ACT(ta, pbh, AF.Tanh, [pn], [tn], scale=0.5)
                                STT(zsil[slot][:, sl], ta, 1.0, pbh, ADD, MUL, [tn, pn], ["zsil%d" % slot])

                        units.append([stA, stB, stC])
                return units

            def post_units(h, par):
                sa = ssa[par]; san = "ssa%d" % par
                units = []

                def r0():
                    TS(sa, sa, 1.0 / 128, EPS, MUL, ADD, [san], [san])
                    ACT(sa, sa, AF.Sqrt, [san], [san])
                    RCP(sa, sa, [san], [san])
                units.append([r0, None, None])
                for st in range(4):
                    hf = ucnt[0] % 2
                    ucnt[0] += 1
                    ob_ = onb2[st % 2]; obn = "onb%d" % (st % 2)
                    ptr = pbank[6 + hf][:, 0:256].bitcast(BF16).rearrange("p (a b) -> p a b", a=4)
                    pn = "pb%d" % (6 + hf)

                    def stA(st=st, ob_=ob_, obn=obn):
                        TT(ob_, ofa[par][:, st], bc(sa[:, st * 4:(st + 1) * 4], S4), MUL, ["ofa%d" % par, san], [obn])

                    def stB(ob_=ob_, obn=obn, ptr=ptr, pn=pn):
                        for ib in range(4):
                            TR(ptr[:, ib, :], ob_[:, ib, :], ident_b[:], [obn, "ident_b"], [pn])

                    def stC(st=st, ptr=ptr, pn=pn):
                        STT(attzT[:, h, st * 512:(st + 1) * 512], ptr.rearrange("p a b -> p (a b)"), sgc[:, 0:1],
                            zsil[par][:, st * 512:(st + 1) * 512], MUL, MUL, [pn, "sgc", "zsil%d" % par], ["attzT"])
                    units.append([stA, stB, stC])
                return units

            class BG:
                def __init__(self, units):
                    self.units = units; self.n = 0

                def pull(self):
                    n = self.n
                    for k, stg in ((n - 2, 2), (n - 1, 1), (n, 0)):
                        if 0 <= k < len(self.units) and self.units[k][stg] is not None:
                            self.units[k][stg]()
                    self.n += 1

                def done(self):
                    return self.n >= len(self.units) + 2

                def drain(self):
                    while not self.done():
                        self.pull()

            steps = [(st, c, J) for st in range(4) for c in range(2) for J in range(4 * st + 4)]

            def emit_qk(h, i):
                st, c, J = steps[i]
                slot = h % 2
                g_ = h * len(steps) + i
                pb = pbank[g_ % 3]; pn = "pb%d" % (g_ % 3)
                jb = J - 4 * st
                c0 = max(0, jb) * 128
                pr_ = slice(c * 64, (c + 1) * 64)
                MM(pb[:, c0:512], kT[slot][c][:, J * 128:(J + 1) * 128], qT[slot][:, st * 512 + c0:(st + 1) * 512],
                   True, (jb < 0 and h >= 2), ["kT%d" % slot, "qT%d" % slot], [pn])
                if jb >= 0:
                    MM(pb[:, c0:c0 + 128], ident_b[:], mask_b[:], False, h >= 2, ["ident_b", "mask_b"], [pn])
                if h < 2:
                    MM(pb[:, c0:512], orow_b[:], cb_b[:, h, c0:512], False, True, ["orow_b", "cb_b"], [pn])

            def emit_exp_pv(h, i):
                st, c, J = steps[i]
                par = h % 2
                g_ = h * len(steps) + i
                pb = pbank[g_ % 3]; pn = "pb%d" % (g_ % 3)
                P_ = Pb[g_ % 3]; Pn = "P%d" % (g_ % 3)
                jb = J - 4 * st
                c0 = max(0, jb) * 128
                Wh = 512
                for s0 in range(0, 512, Wh):
                    a0 = max(s0, c0)
                    if a0 >= s0 + Wh:
                        continue
                    m = (st * 512 + s0 + Wh - 128 * J) // 128
                    ACT(P_[:, a0:s0 + Wh], pb[:, a0:s0 + Wh], AF.Exp, [pn, "bias_tab"], [Pn], bias=bias_tab[:, h, m - 1:m], scale=0.125)
                accA = pbank[3 + c]; an = "pb%d" % (3 + c)
                accB = pbank[5][:, c * 256:c * 256 + 129]; bn = "pb5"
                for ib in range(4):
                    I = 4 * st + ib
                    if I < J:
                        continue
                    if ib < 3:
                        MM(accA[:, ib * 129:(ib + 1) * 129], P_[:, ib * 128:(ib + 1) * 128], v_sb[:, J, h, 0:129],
                           (J == 0 and ib == 0), J == I, [Pn, "v_sb"], [an], skip_group_check=True)
                    else:
                        MM(accB, P_[:, ib * 128:(ib + 1) * 128], v_sb[:, J, h, 0:129],
                           J == 0, J == I, [Pn, "v_sb"], [bn], skip_group_check=True)
                if J == 4 * st + 3:
                    CP(osb[c][:, 0:3, 0:129], accA[:, 0:387].rearrange("p (a b) -> p a b", a=3), [an], ["osb%d" % c])
                    if c == 1:
                        CP(osb[0][:, 3, 0:129], pbank[5][:, 0:129], ["pb5"], ["osb0"])
                        CP(osb[1][:, 3, 0:129], pbank[5][:, 256:385], ["pb5"], ["osb1"])
                        ofn = "ofa%d" % par
                        ofs = ofa[par][:, st]
                        ssl = ssa[par][:, st * 4:(st + 1) * 4]

                        def pc1(ofs=ofs, ofn=ofn):
                            RCP(rr[:, 0:4], osb[0][:, :, 128], ["osb0"], ["rr0"])
                            RCP(rr[:, 4:8], osb[1][:, :, 128], ["osb1"], ["rr1"])
                            TS(rr[:, 4:8], rr[:, 4:8], nlam[:, 0:1], None, MUL, None, ["rr1", "nlam"], ["rr1"])
                            TT(ofs, osb[0][:, :, 0:128], bc(rr[:, 0:4], S4), MUL, ["osb0", "rr0"], [ofn])

                        def pc2(ofs=ofs, ofn=ofn):
                            TT(ot_, osb[1][:, :, 0:128], bc(rr[:, 4:8], S4), MUL, ["osb1", "rr1"], ["ot"])
                            TT(ofs, ofs, ot_, ADD, [ofn, "ot"], [ofn])

                        def pc3(ofs=ofs, ofn=ofn, ssl=ssl, par=par):
                            TT(otmp, ofs, ofs, MUL, [ofn], ["otmp"])
                            A("dve", lambda e: e.tensor_reduce(out=ssl, in_=otmp, axis=AX.X, op=ADD), reads=["otmp"], writes=["ssa%d" % par])
                        dq.extend([pc1, pc2, pc3])

            import collections
            dq = collections.deque()
            BG(proj_units(0, 0)).drain()
            NS = len(steps)
            LOOK = 2
            bg = BG([])
            for g in range(8 * NS):
                h, i = divmod(g, NS)
                if i == 0:
                    while dq:
                        dq.popleft()()
                    bg.drain()
                    us = []
                    if h > 0:
                        us += post_units(h - 1, (h - 1) % 2)
                    if h < 7:
                        us += proj_units(h + 1, (h + 1) % 2)
                    bg = BG(us)
                if g == 0:
                    for la in range(LOOK):
                        emit_qk(0, la)
                gl = g + LOOK
                if gl < 8 * NS:
                    hl, il = divmod(gl, NS)
                    if il < LOOK and hl > h:
                        bg.drain()
                    emit_qk(hl, il)
                emit_exp_pv(h, i)
                if i % 2 == 1:
                    bg.pull()
                elif dq:
                    dq.popleft()()
            while dq:
                dq.popleft()()
            bg.drain()
            BG(post_units(7, 1)).drain()
            wp_cur[0] = wp_main
            S.barrier()
            if dbg and b == 0:
                S.finals.append(DMA("sp", DBG["attzT"], attzT[:, :, :].rearrange("p a b -> p (a b)"), [], [], "dbg"))
                S.finals.append(DMA("sp", DBG["small"], smallf[:], [], [], "dbg"))
                S.barrier()

            o_ = 0
            mergedT = carve(o_, [128, 8, L], BF16); o_ += 8192
            gaf = carve(o_, [128, L], F32); o_ += 2048
            gsf = carve(o_, [128, L], F32); o_ += 2048
            m1f = carve(o_, [128, L], F32); o_ += 2048
            m2 = carve(o_, [128, 512], F32); o_ += 512
            xo2 = [carve(o_ + i * 1024, [128, 1024], F32) for i in range(2)]; o_ += 2048
            xr2 = [carve(o_ + i * 1024, [128, 1024], F32) for i in range(2)]; o_ += 2048
            ob2 = [carve(o_ + i * 1024, [128, 1024], F32) for i in range(2)]; o_ += 2048
            fgb = carve(o_, [128, 1024], F32); o_ += 1024
            wo_sb = carve(o_, [128, 8, 1024], BF16); o_ += 4096
            owb = [carve(o_ + i * 512, [128, 8, 128], BF16) for i in range(6)]; o_ += 3072
            wp_cur[0] = WPool(owb, "owb")
            DMA("sp", fgb, final_g.partition_broadcast(128), [], ["fgb"], "fg")
            for dt_ in range(8):
                for (col0, dstf, dn) in ((5120, gaf, "gaf"), (6144, gsf, "gsf")):
                    def ev_g(n4, pb, pn, dstf=dstf, dn=dn):
                        ACT(dstf[:, n4 * 512:(n4 + 1) * 512], pb[:, :], AF.Tanh, [pn], [dn], scale=0.5)
                    proj_fm(col0 + dt_ * 128, ev_g)
                wa, wan = load_w(w_o_att, dt_ * 128, 128, 8)
                for n4 in range(4):
                    tk = slice(n4 * 512, (n4 + 1) * 512)
                    pb = pbank[n4 % 2]; pn = "pb%d" % (n4 % 2)
                    for hh in range(8):
                        MM(pb[:, :], wa[:, hh, 0:128], attzT[:, hh, tk], hh == 0, hh == 7, [wan, "attzT"], [pn])
                    STT(m1f[:, tk], gaf[:, tk], 1.0, pb[:, :], ADD, MUL, ["gaf", pn], ["m1f"])
                ws, wsn = load_w(w_o_ssm, dt_ * 128, 128, 4)
                for n4 in range(4):
                    tk = slice(n4 * 512, (n4 + 1) * 512)
                    pb = pbank[2 + n4 % 2]; pn = "pb%d" % (2 + n4 % 2)
                    for ft in range(4):
                        MM(pb[:, :], ws[:, ft, 0:128], ssmzT[:, ft, tk], ft == 0, ft == 3, [wsn, "ssmzT"], [pn])
                    STT(m2, gsf[:, tk], 1.0, pb[:, :], ADD, MUL, ["gsf", pn], ["m2"])
                    TT(mergedT[:, dt_, tk], m1f[:, tk], m2, ADD, ["m1f", "m2"], ["mergedT"])
            if dbg and b == 0:
                S.barrier()
                S.finals.append(DMA("sp", DBG["mergedT"], mergedT.rearrange("p a b -> p (a b)"), [], [], "dbg"))
                S.barrier()
            for cb in range(4):
                DMA("pool", wo_sb[:, :, cb * 256:(cb + 1) * 256], w_out[:, cb * 256:(cb + 1) * 256].rearrange("(kt p) c -> p kt c", p=128), [], ["wo_sb"], "wo")
            DMA("sp", xr2[0], x[b, 0:128, :], [], ["xr0"], "xr0")
            for tt in range(16):
                sl_ = tt % 2
                xr = xr2[sl_]; xo = xo2[sl_]; ob = ob2[sl_]
                xrn = "xr%d" % sl_; xon = "xo%d" % sl_; obn = "ob%d" % sl_; sqn = "ssqf%d" % sl_
                sq_ = smallf[:, 60 + 2 * sl_:62 + 2 * sl_]
                if tt + 1 < 16:
                    DMA("sp", xr2[1 - sl_], x[b, (tt + 1) * 128:(tt + 2) * 128, :], [], ["xr%d" % (1 - sl_)], "xr%d" % (1 - sl_))
                for hf in range(2):
                    pb = pbank[4 + hf + 2 * sl_]; pn = "pb%d" % (4 + hf + 2 * sl_)
                    for dt_ in range(8):
                        MM(pb[:, :], mergedT[:, dt_, tt * 128:(tt + 1) * 128], wo_sb[:, dt_, hf * 512:(hf + 1) * 512], dt_ == 0, dt_ == 7, ["mergedT", "wo_sb"], [pn])
                    STT(xo[:, hf * 512:(hf + 1) * 512], pb[:, :], 0.5, xr[:, hf * 512:(hf + 1) * 512], MUL, ADD, [pn, xrn], [xon])
                ACT(ob, xo, AF.Square, [xon], [obn, sqn], accum=sq_[:, 0:1])
                TS(sq_[:, 1:2], sq_[:, 0:1], 1.0 / D, EPS, MUL, ADD, [sqn], [sqn + "b"])
                ACT(sq_[:, 1:2], sq_[:, 1:2], AF.Sqrt, [sqn + "b"], [sqn + "b"])
                RCP(sq_[:, 1:2], sq_[:, 1:2], [sqn + "b"], [sqn + "b"])
                STT(ob, xo, sq_[:, 1:2], fgb, MUL, MUL, [xon, sqn + "b", "fgb", obn], [obn])
                fin = DMA("sp", out[b, tt * 128:(tt + 1) * 128, :], ob, [obn], [], "out")
                S.finals.append(fin)
            wp_cur[0] = wp_main
            S.barrier()
        S.emit()
    return nc


_CACHE = {}


def kernel(**inputs):
    if "nc" not in _CACHE:
        _CACHE["nc"] = build_program()
    nc = _CACHE["nc"]
    f = lambda a: np.ascontiguousarray(np.asarray(a, dtype=np.float32))
    x = f(inputs["x"])
    common = {
        "norm_g": f(inputs["norm_g"]).reshape(D),
        "w_in": f(inputs["w_in"]).reshape(D, 7168),
        "lambda_q1": f(inputs["lambda_q1"]).reshape(64), "lambda_k1": f(inputs["lambda_k1"]).reshape(64),
        "lambda_q2": f(inputs["lambda_q2"]).reshape(64), "lambda_k2": f(inputs["lambda_k2"]).reshape(64),
        "subln_g": f(inputs["subln_g"]).reshape(128),
        "w_o_att": f(inputs["w_o_att"]).reshape(D, D),
        "ssm_lambda_re": f(inputs["ssm_lambda_re"]).reshape(32, 64), "ssm_lambda_im": f(inputs["ssm_lambda_im"]).reshape(32, 64),
        "ssm_log_dt": f(inputs["ssm_log_dt"]).reshape(32),
        "ssm_b_re": f(inputs["ssm_b_re"]).reshape(32, 64, 16), "ssm_b_im": f(inputs["ssm_b_im"]).reshape(32, 64, 16),
        "ssm_c_re": f(inputs["ssm_c_re"]).reshape(32, 16, 64), "ssm_c_im": f(inputs["ssm_c_im"]).reshape(32, 16, 64),
        "ssm_d": f(inputs["ssm_d"]).reshape(512),
        "w_glu": f(inputs["w_glu"]).reshape(512, 512), "b_glu": f(inputs["b_glu"]).reshape(512),
        "w_o_ssm": f(inputs["w_o_ssm"]).reshape(512, D), "w_out": f(inputs["w_out"]).reshape(D, D),
        "final_g": f(inputs["final_g"]).reshape(D),
    }
    in_maps = []
    for c in range(8):
        m = dict(common)
        m["x"] = np.ascontiguousarray(x[2 * c:2 * c + 2])
        in_maps.append(m)
    res = run_bass_kernel_spmd(nc, in_maps, core_ids=list(range(8)))
    return np.concatenate([np.asarray(r["out"]) for r in res.results], axis=0).astype(np.float32)
```
